# Optimizing a Trainium2 kernel written in Bass

```python
import math
import jax, jax.numpy as jnp
from jax import lax
import numpy as np

D_MODEL = 2048
BATCH = 4
SEQ = 8192
DEPTH = 1

CHUNK = 64
N_META = 16
Q_BLOCK = 128
MIX_WIDTH = D_MODEL
ATTN_WIDTH = MIX_WIDTH // 2
POOL_WIDTH = MIX_WIDTH - ATTN_WIDTH
N_ATTN_HEADS = 8
QK_HEAD_DIM = ATTN_WIDTH // (2 * N_ATTN_HEADS)
V_HEAD_DIM = 2 * QK_HEAD_DIM
POOL_WINDOWS = (2, 4, 8, 16)
N_POOL_GROUPS = len(POOL_WINDOWS)
POOL_GROUP_WIDTH = POOL_WIDTH // N_POOL_GROUPS
IN_PROJ_WIDTH = 3 * ATTN_WIDTH + POOL_WIDTH
D_FF = 5632
LN_EPS = 1e-5
DEEPNORM_ALPHA = (2.0 * DEPTH) ** 0.25
DEEPNORM_BETA = (8.0 * DEPTH) ** -0.25
NEG_INF = -1e30

kernel_name = 'hymba_diffattn_pool_macaron_deepnorm'


def layer_norm(x, g, b):
    xf = x.astype(jnp.float32)
    mu = jnp.mean(xf, axis=-1, keepdims=True)
    var = jnp.mean(jnp.square(xf - mu), axis=-1, keepdims=True)
    y = (xf - mu) * lax.rsqrt(var + LN_EPS)
    return (y * g.astype(jnp.float32) + b.astype(jnp.float32)).astype(x.dtype)


def swiglu(x, w_gate, w_up, w_down):
    return (jax.nn.silu(x @ w_gate) * (x @ w_up)) @ w_down


def chunk_ids(pos):
    return jnp.where(pos < N_META, 0, 1 + (pos - N_META) // CHUNK)


def alibi_slopes(n_heads):
    return 2.0 ** (-8.0 * jnp.arange(1, n_heads + 1, dtype=jnp.float32) / n_heads)


def diff_attention(q, k, v, lam, lam_init, subln_g):
    B, Lp, H, _, dk = q.shape
    dv = v.shape[-1]
    nb = Lp // Q_BLOCK
    pos = jnp.arange(Lp, dtype=jnp.int32)
    cid = chunk_ids(pos)
    slopes = alibi_slopes(H)
    scale = dk ** -0.5
    qb = q.reshape(B, nb, Q_BLOCK, H, 2, dk).transpose(1, 0, 3, 4, 2, 5)
    kt = k.transpose(0, 2, 3, 1, 4)
    vt = v.transpose(0, 2, 1, 3)
    pos_b = pos.reshape(nb, Q_BLOCK)

    def block(args):
        q_blk, qpos = args
        s = jnp.einsum('bhmqd,bhmkd->bhmqk', q_blk, kt,
                       preferred_element_type=jnp.float32) * scale
        dist = jnp.abs(qpos[:, None] - pos[None, :]).astype(jnp.float32)
        s = s - slopes[None, :, None, None, None] * dist[None, None, None]
        visible = chunk_ids(qpos)[:, None] >= cid[None, :]
        s = jnp.where(visible[None, None, None], s, NEG_INF)
        p = jax.nn.softmax(s, axis=-1)
        a = p[:, :, 0] - lam * p[:, :, 1]
        return jnp.einsum('bhqk,bhkd->bhqd', a.astype(vt.dtype), vt)

    o = lax.map(block, (qb, pos_b))
    o = o.transpose(1, 0, 3, 2, 4).reshape(B, Lp, H, dv).astype(jnp.float32)
    o = o * lax.rsqrt(jnp.mean(jnp.square(o), axis=-1, keepdims=True) + LN_EPS)
    o = o * subln_g.astype(jnp.float32) * (1.0 - lam_init)
    return o.astype(q.dtype).reshape(B, Lp, H * dv)


def multiscale_pool(u, w_pool, pool_scale):
    B, Lp, _ = u.shape
    ug = u.reshape(B, Lp, N_POOL_GROUPS, POOL_GROUP_WIDTH)
    cs = jnp.cumsum(ug.astype(jnp.float32), axis=1)
    t1 = jnp.arange(1, Lp + 1, dtype=jnp.float32)
    pooled = []
    for g, w in enumerate(POOL_WINDOWS):
        csg = cs[:, :, g]
        prev = jnp.pad(csg, ((0, 0), (w, 0), (0, 0)))[:, :Lp]
        count = jnp.minimum(t1, float(w))
        mean = (csg - prev) / count[None, :, None]
        pooled.append(mean - ug[:, :, g].astype(jnp.float32))
    pooled = jnp.stack(pooled, axis=2).astype(u.dtype)
    y = jnp.einsum('blgc,gcd->blgd', pooled, w_pool)
    return y.reshape(B, Lp, POOL_WIDTH) * pool_scale


def setup_inputs(seed: int = 0) -> dict:
    key = jax.random.key(seed)
    ks = jax.random.split(key, 24)
    f32 = jnp.float32
    nrm = lambda k, shape, s: jax.random.normal(k, shape, f32) * s
    return {
        'x': nrm(ks[0], (BATCH, SEQ, D_MODEL), 1.0),
        'meta_tokens': nrm(ks[1], (N_META, D_MODEL), 1.0),
        'ln1_g': 1.0 + nrm(ks[2], (DEPTH, D_MODEL), 0.02),
        'ln1_b': nrm(ks[3], (DEPTH, D_MODEL), 0.02),
        'ffn1_w_gate': nrm(ks[4], (DEPTH, D_MODEL, D_FF), D_MODEL ** -0.5),
        'ffn1_w_up': nrm(ks[5], (DEPTH, D_MODEL, D_FF), D_MODEL ** -0.5),
        'ffn1_w_down': nrm(ks[6], (DEPTH, D_FF, D_MODEL), DEEPNORM_BETA * D_FF ** -0.5),
        'w_in': nrm(ks[7], (DEPTH, D_MODEL, IN_PROJ_WIDTH), D_MODEL ** -0.5),
        'lambda_q1': nrm(ks[8], (DEPTH, QK_HEAD_DIM), 0.1),
        'lambda_k1': nrm(ks[9], (DEPTH, QK_HEAD_DIM), 0.1),
        'lambda_q2': nrm(ks[10], (DEPTH, QK_HEAD_DIM), 0.1),
        'lambda_k2': nrm(ks[11], (DEPTH, QK_HEAD_DIM), 0.1),
        'subln_g': 1.0 + nrm(ks[12], (DEPTH, V_HEAD_DIM), 0.02),
        'w_pool': nrm(ks[13], (DEPTH, N_POOL_GROUPS, POOL_GROUP_WIDTH, POOL_GROUP_WIDTH), POOL_GROUP_WIDTH ** -0.5),
        'pool_scale': 1.0 + nrm(ks[14], (DEPTH, POOL_WIDTH), 0.1),
        'w_out': nrm(ks[15], (DEPTH, MIX_WIDTH, D_MODEL), DEEPNORM_BETA * MIX_WIDTH ** -0.5),
        'ln2_g': 1.0 + nrm(ks[16], (DEPTH, D_MODEL), 0.02),
        'ln2_b': nrm(ks[17], (DEPTH, D_MODEL), 0.02),
        'ffn2_w_gate': nrm(ks[18], (DEPTH, D_MODEL, D_FF), D_MODEL ** -0.5),
        'ffn2_w_up': nrm(ks[19], (DEPTH, D_MODEL, D_FF), D_MODEL ** -0.5),
        'ffn2_w_down': nrm(ks[20], (DEPTH, D_FF, D_MODEL), DEEPNORM_BETA * D_FF ** -0.5),
        'ln3_g': 1.0 + nrm(ks[21], (DEPTH, D_MODEL), 0.02),
        'ln3_b': nrm(ks[22], (DEPTH, D_MODEL), 0.02),
    }


def reference(x, meta_tokens, ln1_g, ln1_b, ffn1_w_gate, ffn1_w_up, ffn1_w_down,
              w_in, lambda_q1, lambda_k1, lambda_q2, lambda_k2, subln_g, w_pool,
              pool_scale, w_out, ln2_g, ln2_b, ffn2_w_gate, ffn2_w_up, ffn2_w_down,
              ln3_g, ln3_b):
    B, S, D = x.shape
    L = S + N_META
    Lp = ((L + Q_BLOCK - 1) // Q_BLOCK) * Q_BLOCK
    meta = jnp.broadcast_to(meta_tokens.astype(x.dtype)[None], (B, N_META, D))
    pad = jnp.zeros((B, Lp - L, D), x.dtype)
    h = jnp.concatenate([meta, x, pad], axis=1)
    H, dk = N_ATTN_HEADS, QK_HEAD_DIM
    for l in range(DEPTH):
        h = layer_norm(DEEPNORM_ALPHA * h + 0.5 * swiglu(h, ffn1_w_gate[l], ffn1_w_up[l], ffn1_w_down[l]),
                       ln1_g[l], ln1_b[l])
        xw = h @ w_in[l]
        q = xw[..., :ATTN_WIDTH].reshape(B, Lp, H, 2, dk)
        k = xw[..., ATTN_WIDTH:2 * ATTN_WIDTH].reshape(B, Lp, H, 2, dk)
        v = xw[..., 2 * ATTN_WIDTH:3 * ATTN_WIDTH].reshape(B, Lp, H, V_HEAD_DIM)
        u = xw[..., 3 * ATTN_WIDTH:]
        lam_init = 0.8 - 0.6 * math.exp(-0.3 * l)
        lam = (jnp.exp(jnp.sum(lambda_q1[l].astype(jnp.float32) * lambda_k1[l].astype(jnp.float32)))
               - jnp.exp(jnp.sum(lambda_q2[l].astype(jnp.float32) * lambda_k2[l].astype(jnp.float32)))
               + lam_init)
        a_out = diff_attention(q, k, v, lam, lam_init, subln_g[l])
        p_out = multiscale_pool(u, w_pool[l], pool_scale[l])
        mix = jnp.concatenate([a_out, p_out.astype(a_out.dtype)], axis=-1) @ w_out[l]
        h = layer_norm(DEEPNORM_ALPHA * h + mix, ln2_g[l], ln2_b[l])
        h = layer_norm(DEEPNORM_ALPHA * h + 0.5 * swiglu(h, ffn2_w_gate[l], ffn2_w_up[l], ffn2_w_down[l]),
                       ln3_g[l], ln3_b[l])
    return h[:, N_META:N_META + S]
```

```python
import math
from contextlib import ExitStack

import numpy as np
import concourse.bass as bass
import concourse.mybir as mybir
from concourse.bass_utils import run_bass_kernel_spmd

F32 = mybir.dt.float32
BF16 = mybir.dt.bfloat16
AF = mybir.ActivationFunctionType
OP = mybir.AluOpType
AX = mybir.AxisListType

D = 2048
DFF = 5632
SEQ = 8192
NMETA = 16
NH = 8
T = 512
NOWN = 4096
NTILE = NOWN // T
NQT = NOWN // 128
NKT = 1 + 2 * NQT
NKTOK = NKT * 128
KC = D // 128
FC = DFF // 128
ALPHA = 2.0 ** 0.25
LN_EPS = 1e-5
EPS_S = LN_EPS / (ALPHA * ALPHA)
LAM_INIT = 0.8 - 0.6 * math.exp(-0.3 * 0)
NEG = -30000.0
NW = 4

ENGS = ["sync", "scalar", "vector", "gpsimd", "tensor"]
SEM_NAMES = (["wld%d" % i for i in range(NW)] +
             ["wfree", "pe_tr", "pe_m", "xin0", "xin1", "res0", "res1", "res2", "st_q0", "st_q1", "st_q2", "st_q3", "act_a", "dve_a", "act_b", "dve_b",
              "st_h0", "st_h1", "st_h2", "st_q", "st_k", "st_v", "st_p", "st_u", "st_a", "st_y0", "st_y1", "st_y2", "cst_g", "cst_s",
              "kv0", "kv1", "ld_a", "ld_p", "ld_uh", "ld_ln", "phase"])


class Pass:
    def __init__(self, name, sems):
        self.name = name
        self.e = None
        self.sems = sems
        self.cnt = {}
        self.waited = {}

    def emit(self, eng, fn, waits=(), sig=None, amt=1):
        ev = None
        if sig is not None:
            self.cnt[sig] = self.cnt.get(sig, 0) + amt
            ev = (sig, self.cnt[sig])
        if self.name == eng:
            self._wait(waits)
            ins = fn(self.e)
            if sig is not None:
                ins.then_inc(self.sems[sig], amt)
        return ev

    def _wait(self, waits):
        for w in waits:
            if w is None:
                continue
            if isinstance(w, list):
                self._wait(w)
                continue
            s, v = w
            if self.waited.get(s, 0) < v:
                self.e.wait_ge(self.sems[s], v)
                self.waited[s] = v

    def dma(self, q, out, in_, sig, waits=()):
        return self.emit(q, lambda e: e.dma_start(out=out, in_=in_), waits, sig, 16)

    def last(self, sig):
        return (sig, self.cnt.get(sig, 0)) if self.cnt.get(sig, 0) > 0 else None


def build(stage=3, debug=False):
    nc = bass.Bass("TRN2", target_bir_lowering=False)
    es = ExitStack()

    def din(name, shape):
        return nc.dram_tensor(name, list(shape), F32, kind="ExternalInput").ap()

    def dscr(name, shape, dt):
        kind = "ExternalOutput" if debug else "Internal"
        return nc.dram_tensor(name, list(shape), dt, kind=kind).ap()

    x_own = din("x_own", [NOWN, D]); x_oth = din("x_oth", [NOWN, D])
    x_halo = din("x_halo", [T, D])
    qaug = din("qaug", [NH, 3, NOWN]); kaug = din("kaug", [3, NKTOK])
    ident_d = din("ident", [128, 128]); kbias_d = din("kbias", [128, NH])
    corr_d = din("corr", [128, NH * 128]); corr2_d = din("corr2", [128, 128])
    pscale_d = din("pscale", [128, 8])
    ln_d = {k: din(k, [1, D]) for k in ["ln1_g", "ln1_b", "ln2_g", "ln2_b", "ln3_g", "ln3_b"]}
    w1g = din("ffn1_w_gate", [D, DFF]); w1u = din("ffn1_w_up", [D, DFF]); w1d = din("ffn1_w_down", [DFF, D])
    w2g = din("ffn2_w_gate", [D, DFF]); w2u = din("ffn2_w_up", [D, DFF]); w2d = din("ffn2_w_down", [DFF, D])
    w_in = din("w_in", [D, 4096]); w_out = din("w_out", [D, D]); w_pool = din("w_pool", [4, 256, 256])
    lam_d = {k: din(k, [1, 64]) for k in ["lambda_q1", "lambda_k1", "lambda_q2", "lambda_k2"]}
    subln_d = din("subln_g", [1, 128])
    y = nc.dram_tensor("y", [NOWN, D], F32, kind="ExternalOutput").ap()

    h1s = dscr("h1s", [NOWN, D], F32); h2s = dscr("h2s", [NOWN, D], F32)
    qs = dscr("qs", [NH, 2, 67, NOWN], BF16); ks = dscr("ks", [NH, 2, 67, NKTOK], BF16)
    vs = dscr("vs", [NH, 128, NKT, 129], BF16); ps = dscr("ps", [NTILE, 128, 8, T], BF16)
    uhs = dscr("uhs", [128, 8, T], F32); as_ = dscr("as_", [NQT, 128, NH * 128], BF16)

    def wview(w):
        return w.rearrange("(k p) f -> p k f", p=128)

    sems = {n: es.enter_context(nc.semaphore(n)) for n in SEM_NAMES}
    passes = {n: Pass(n, sems) for n in ENGS}

    def sb(name, shape, dt):
        return es.enter_context(nc.sbuf_tensor("s_" + name, list(shape), dt))

    ident_b = sb("ident_b", [128, 128], BF16); ident_f = sb("ident_f", [128, 128], F32)
    kbias = sb("kbias", [128, NH], F32); corr = sb("corr", [128, NH * 128], BF16)
    corr2 = sb("corr2", [128, 128], BF16); pscale = sb("pscale", [128, 8], F32)
    neglam = sb("neglam", [128, 1], F32); gsub = sb("gsub", [128, 128], F32)
    lamt = sb("lamt", [128, 256], F32); lamr = sb("lamr", [128, 4], F32)
    wpool = sb("wpool", [128, 8 * 256], BF16)
    lng = sb("lng", [128, D], F32); lnb = sb("lnb", [128, D], F32)
    zero_t = sb("zero_t", [128, 256], BF16)
    stats = sb("stats", [128, 72], F32); mv = sb("mv", [128, 24], F32); epst = sb("epst", [128, 2], F32)
    BIG = 180 * 1024
    big = sb("big", [128, BIG // 2], BF16)
    banks = [es.enter_context(nc.psum_tensor("bank%d" % i, [128, 512], F32)) for i in range(8)]

    off = {"o": 0}

    def carve(shape, dt, reset=None):
        if reset is not None:
            off["o"] = reset
        n = 1
        for s_ in shape[1:]:
            n *= s_
        nb = n * (2 if dt == BF16 else 4)
        o = off["o"]
        assert o % 4 == 0
        off["o"] = o + ((nb + 31) // 32) * 32
        assert off["o"] <= BIG, ("sbuf big overflow", off["o"])
        ap = big[0:shape[0], o // 2:o // 2 + nb // 2]
        if dt == F32:
            ap = ap.bitcast(F32)
        if len(shape) == 3:
            ap = ap.rearrange("p (a b) -> p a b", a=shape[1])
        elif len(shape) == 4:
            ap = ap.rearrange("p (a b c) -> p a b c", a=shape[1], b=shape[2])
        return ap

    hT = carve([128, KC, T], BF16, reset=0)
    actT = carve([128, FC, T], BF16)
    a0 = off["o"] - FC * T * 2
    wsl = [carve([128, 4096], BF16) for _ in range(NW)]
    fT = carve([128, KC, T], F32)
    f0 = off["o"] - KC * T * 4
    xin = [carve([128, D], BF16) for _ in range(2)]
    zts = [carve([128, D], F32) for _ in range(3)]
    hb = carve([128, D], BF16)
    sg = [carve([128, T], F32) for _ in range(2)]
    kst = carve([128, NH, T], BF16)
    qsb = [carve([128, T], BF16) for _ in range(4)]
    vsb = [carve([128, T], BF16) for _ in range(2)]
    ffn_end = off["o"]
    u_sb = carve([128, 8, 4, 144], F32, reset=a0)
    v_stage = carve([128, NH, 4, 129], BF16)
    pooled = carve([128, 8, T], BF16)
    pout = carve([128, 8, T], BF16)
    assert off["o"] <= a0 + FC * T * 2, off["o"] - a0
    mixT = carve([128, KC, T], BF16, reset=a0)
    a_tok = carve([128, 4, NH * 128], BF16)
    wA = carve([128, 8, 4, 144], F32, reset=f0)
    wB = carve([128, 6, 4, 144], F32)
    assert off["o"] <= f0 + KC * T * 4
    Kb = [carve([128, 2, NKTOK], BF16, reset=(0 if i == 0 else None)) for i in range(2)]
    Vb = [carve([128, NKT, 129], BF16) for _ in range(2)]
    Qb = [carve([128, 2, NOWN], BF16) for _ in range(2)]
    aoh = [carve([128, NQT, 128], BF16) for _ in range(2)]
    PT = [carve([128, 512], BF16) for _ in range(3)]
    o1 = [carve([128, 128], F32) for _ in range(4)]
    o2 = [carve([128, 128], F32) for _ in range(4)]
    rz = [carve([128, 8], F32) for _ in range(4)]
    att_end = off["o"]

    def barrier(E, extra=()):
        n = E.cnt.get("phase", 0)
        tgt = n + len(ENGS)
        for eng in ENGS:
            E.emit(eng, lambda e: e.nop(), extra, "phase")
        for eng in ENGS:
            E.emit(eng, lambda e: e.nop(), [("phase", tgt)])

    def prologue(E):
        E.dma("gpsimd", ident_b[:, :], ident_d, "cst_g")
        E.dma("sync", ident_f[:, :], ident_d, "cst_s")
        E.dma("sync", kbias[:, :], kbias_d, "cst_s")
        E.dma("gpsimd", corr[:, :], corr_d, "cst_g")
        E.dma("gpsimd", corr2[:, :], corr2_d, "cst_g")
        E.dma("sync", pscale[:, :], pscale_d, "cst_s")
        E.dma("gpsimd", wpool[:, :].rearrange("p (a d) -> p a d", a=8),
              w_pool.rearrange("g (c p) d -> p (g c) d", p=128), "cst_g")
        for i, k in enumerate(["lambda_q1", "lambda_k1", "lambda_q2", "lambda_k2"]):
            E.dma("sync", lamt[:, i * 64:(i + 1) * 64], lam_d[k].broadcast_to([128, 64]), "cst_s")
        cst = E.dma("sync", gsub[:, :], subln_d.broadcast_to([128, 128]), "cst_s")
        E.emit("vector", lambda e: e.memset(epst[:, 0:1], EPS_S), (), "dve_a")
        E.emit("vector", lambda e: e.memset(epst[:, 1:2], LN_EPS), (), "dve_a")
        e0 = E.emit("vector", lambda e: e.memset(zero_t[:, :], 0.0), (), "dve_a")
        for h in range(NH):
            for m in range(2):
                E.dma("sync", ks[h, m, :, 0:128], zero_t[0:67, 0:128], "st_k", [e0])
        kz = E.last("st_k")
        for h in range(NH):
            E.dma("sync", vs[h, :, 0, :], zero_t[:, 0:129], "st_v", [e0])
        for h in range(NH):
            for m in range(2):
                E.dma("gpsimd", ks[h, m, 64:67, :], kaug, "cst_g", [kz])
                E.dma("gpsimd", qs[h, m, 64:67, :], qaug[h], "cst_g")
        a1 = E.emit("vector", lambda e: e.tensor_tensor(out=lamt[:, 0:64], in0=lamt[:, 0:64], in1=lamt[:, 64:128], op=OP.mult), [cst], "dve_a")
        a2 = E.emit("vector", lambda e: e.tensor_tensor(out=lamt[:, 128:192], in0=lamt[:, 128:192], in1=lamt[:, 192:256], op=OP.mult), [a1], "dve_a")
        a3 = E.emit("vector", lambda e: e.reduce_sum(out=lamr[:, 0:1], in_=lamt[:, 0:64], axis=AX.X), [a2], "dve_a")
        a4 = E.emit("vector", lambda e: e.reduce_sum(out=lamr[:, 1:2], in_=lamt[:, 128:192], axis=AX.X), [a3], "dve_a")
        b1 = E.emit("scalar", lambda e: e.activation(out=lamr[:, 2:4], in_=lamr[:, 0:2], func=AF.Exp), [a4], "act_a")
        a5 = E.emit("vector", lambda e: e.tensor_tensor(out=neglam[:, :], in0=lamr[:, 3:4], in1=lamr[:, 2:3], op=OP.subtract), [b1], "dve_a")
        a6 = E.emit("vector", lambda e: e.tensor_scalar(out=neglam[:, :], in0=neglam[:, :], scalar1=-LAM_INIT, scalar2=None, op0=OP.add), [a5], "dve_a")
        E.emit("vector", lambda e: e.tensor_scalar(out=gsub[:, :], in0=gsub[:, :], scalar1=1.0 - LAM_INIT, scalar2=None, op0=OP.mult), [a6], "dve_a")
        barrier(E, [E.last("cst_g"), E.last("cst_s")])

    def wr(E):
        if not hasattr(E, "wr_"):
            E.wr_ = {"plan": [], "issued": 0, "taken": 0, "evs": {}, "bp": 0, "bpfree": {}, "tb": 0, "tbfree": {}}
        return E.wr_

    def wpump(E):
        w = wr(E)
        while w["issued"] < len(w["plan"]) and w["issued"] - w["taken"] < NW:
            k = w["issued"]
            src, nk = w["plan"][k]
            slot = k % NW
            waits = [("wfree", k - NW + 1)] if k >= NW else []
            dst = wsl[slot][:, 0:nk * 256].rearrange("p (k c) -> p k c", k=nk)
            w["evs"][k] = E.dma("gpsimd", dst, src, "wld%d" % slot, waits)
            w["issued"] += 1

    def proj_plan(E, Wv, nk, cols):
        for c0 in cols:
            k0 = 0
            while k0 < nk:
                n = min(16, nk - k0)
                wr(E)["plan"].append((Wv[:, k0:k0 + n, c0:c0 + 256], n))
                k0 += n

    def next_pair(E):
        w = wr(E)
        r = w["bp"] % 3
        w["bp"] += 1
        return r, w["bpfree"].get(r)

    def next_tb(E):
        w = wr(E)
        r = w["tb"] % 2
        w["tb"] += 1
        return r, w["tbfree"].get(r)

    def mm(E, out, lhsT, rhs, start, stop, waits=(), sig=None, skip=False):
        return E.emit("tensor", lambda e: e.matmul(out, lhsT, rhs, start=start, stop=stop, skip_group_check=skip), waits, sig)

    def proj_run(E, nk, n_ocp, srcs, Tn, src_waits, epilogue):
        w = wr(E)
        for i in range(n_ocp):
            r, bfree = next_pair(E)
            bks = [banks[2 * r], banks[2 * r + 1]]
            k0 = 0
            ev = None
            while k0 < nk:
                n = min(16, nk - k0)
                wpump(E)
                k = w["taken"]
                assert k < w["issued"]
                w["taken"] += 1
                slot = k % NW
                wev = w["evs"].pop(k)
                wt = wsl[slot][:, 0:n * 256].rearrange("p (k c) -> p k c", k=n)
                for oc in range(2):
                    for kk in range(n):
                        kc = k0 + kk
                        waits = []
                        if kk == 0 and oc == 0:
                            waits = [wev]
                            if kc == 0:
                                waits += [bfree, list(src_waits)]
                        lastmm = (oc == 1 and kk == n - 1)
                        ev_ = mm(E, bks[oc][:, 0:Tn], wt[:, kk, oc * 128:(oc + 1) * 128], srcs[kc],
                                 kc == 0, kc == nk - 1, waits, "wfree" if lastmm else None)
                        if lastmm:
                            ev = ev_
                k0 += n
                wpump(E)
            w["bpfree"][r] = epilogue(i, r, bks, ev)

    def tr_in(E, src_tok, src_ev, dst, col0, nchunks=KC):
        w = wr(E)
        evs = []
        pe_last = None
        for g in range(nchunks // 8):
            r, tfree = next_tb(E)
            tbv = banks[6 + r][:, :].bitcast(BF16)
            for i in range(8):
                c = g * 8 + i
                waits = [src_ev, tfree] if i == 0 else []
                pe_last = E.emit("tensor", lambda e, c=c, i=i: e.transpose(tbv[:, i * 128:(i + 1) * 128], src_tok[:, c * 128:(c + 1) * 128], ident_b[:, :]),
                                 waits, "pe_tr" if i == 7 else None)
            eng, sgn = (("vector", "dve_a") if g % 2 == 0 else ("scalar", "act_a"))
            src3 = tbv.rearrange("p (k c) -> p k c", k=8)
            dst3 = dst[:, g * 8:(g + 1) * 8, col0:col0 + 128]
            if eng == "vector":
                ev = E.emit("vector", lambda e: e.tensor_copy(out=dst3, in_=src3), [pe_last], sgn)
            else:
                ev = E.emit("scalar", lambda e: e.activation(out=dst3, in_=src3, func=AF.Copy), [pe_last], sgn)
            w["tbfree"][r] = ev
            evs.append(ev)
        return evs, pe_last

    def ln_epilogue(E, ns, fT_evs, resid_src, cscale, eps, ln_ev, consumer, res_waits=None):
        w = wr(E)
        st = E.__dict__.setdefault("ln_st", {"free": [None, None, None], "zc": 0})
        info = {}
        loads = {}

        def load(s):
            zb = st["zc"] % 3
            st["zc"] += 1
            loads[s] = (zb, E.dma("sync", zts[zb][:, :], resid_src(s), "res%d" % zb, [st["free"][zb], res_waits]))

        def stage_a(s):
            zb, rev = loads.pop(s)
            z = zts[zb]
            sts = stats[:, zb * 24:(zb + 1) * 24]
            m_ = mv[:, zb * 8:(zb + 1) * 8]
            zevs = []
            for qd in range(4):
                r, tfree = next_tb(E)
                tbk = banks[6 + r]
                pe = None
                for i in range(4):
                    waits = [list(fT_evs), tfree] if i == 0 else []
                    pe = E.emit("tensor", lambda e, i=i, qd=qd: e.transpose(tbk[:, i * 128:(i + 1) * 128], fT[:, 4 * qd + i, s * 128:(s + 1) * 128], ident_f[:, :]),
                                waits, "pe_tr" if i == 3 else None)
                zsl = z[:, qd * 512:(qd + 1) * 512]
                ev = E.emit("vector", lambda e: e.scalar_tensor_tensor(out=zsl, in0=tbk[:, :], scalar=cscale, in1=zsl, op0=OP.mult, op1=OP.add),
                            [pe, rev], "dve_a")
                w["tbfree"][r] = ev
                ev2 = E.emit("vector", lambda e, qd=qd: e.bn_stats(out=sts[:, qd * 6:(qd + 1) * 6], in_=zsl), [ev], "dve_a")
                zevs.append(ev2)
            e1 = E.emit("vector", lambda e: e.bn_aggr(out=m_[:, 0:2], in_=sts), [zevs[-1]], "dve_a")
            e2a = E.emit("scalar", lambda e: e.activation(out=m_[:, 4:5], in_=m_[:, 1:2], func=AF.Ln, bias=epst[:, 0:1], scale=1.0), [e1], "act_a")
            e2 = E.emit("scalar", lambda e: e.activation(out=m_[:, 2:3], in_=m_[:, 4:5], func=AF.Exp, scale=-0.5), [e2a], "act_a")
            e3 = E.emit("vector", lambda e: e.scalar_tensor_tensor(out=m_[:, 3:4], in0=m_[:, 0:1], scalar=-1.0, in1=m_[:, 2:3], op0=OP.mult, op1=OP.mult), [e2], "dve_a")
            e4 = E.emit("scalar", lambda e: e.activation(out=z[:, :], in_=z[:, :], func=AF.Identity, bias=m_[:, 3:4], scale=m_[:, 2:3]), [e3], "act_a")
            info[s] = (zb, z, e4)

        posts = {}

        def stage_c(s):
            zb, z, e4 = info.pop(s)
            e5 = E.emit("vector", lambda e: e.tensor_tensor(out=z[:, :], in0=z[:, :], in1=lng[:, :], op=OP.mult), [e4, ln_ev], "dve_a")
            e6 = E.emit("vector", lambda e: e.tensor_tensor(out=z[:, :], in0=z[:, :], in1=lnb[:, :], op=OP.add), [e5], "dve_a")
            res = consumer(s, e6, z, zb)
            if isinstance(res, tuple):
                st["free"][zb], posts[s] = res
            else:
                st["free"][zb] = res

        for s in range(min(3, ns)):
            load(s)
        stage_a(0)
        if ns > 1:
            stage_a(1)
        for s in range(ns):
            stage_c(s)
            if s + 2 < ns:
                if s + 2 >= 3:
                    load(s + 2)
                stage_a(s + 2)
            if s in posts:
                posts.pop(s)()

    def load_ln(E, gk, bk, waits=()):
        E.dma("sync", lng[:, :], ln_d[gk].broadcast_to([128, D]), "ld_ln", waits)
        return E.dma("sync", lnb[:, :], ln_d[bk].broadcast_to([128, D]), "ld_ln", waits)

    def ffn_plan(E, wg, wu, wd):
        wgv, wuv, wdv = wview(wg), wview(wu), wview(wd)
        for j in range(FC // 2):
            proj_plan(E, wgv, KC, [j * 256])
            proj_plan(E, wuv, KC, [j * 256])
        proj_plan(E, wdv, FC, [i * 256 for i in range(8)])

    def ffn_run(E, Tn, hT_evs, act_waits=None):
        st = {"g": None, "sgfree": [None, None], "mul": []}
        srcs = [hT[:, kc, 0:Tn] for kc in range(KC)]

        def gu_epi(i, r, bks, ev):
            if i % 2 == 0:
                st["g"] = (r, bks, ev)
                return wr(E)["bpfree"].get(r)
            j = i // 2
            rg, gb, gev = st["g"]
            afree, dfree = [], []
            for fc in range(2):
                a = E.emit("scalar", lambda e, fc=fc: e.activation(out=sg[fc][:, 0:Tn], in_=gb[fc][:, 0:Tn], func=AF.Silu),
                           [gev, st["sgfree"][fc]], "act_b")
                d = E.emit("vector", lambda e, fc=fc: e.tensor_tensor(out=actT[:, 2 * j + fc, 0:Tn], in0=sg[fc][:, 0:Tn], in1=bks[fc][:, 0:Tn], op=OP.mult),
                           [a, ev, act_waits], "dve_b")
                st["sgfree"][fc] = d
                afree.append(a); dfree.append(d)
            wr(E)["bpfree"][rg] = afree
            st["mul"] = dfree
            return dfree

        proj_run(E, KC, FC, srcs, Tn, hT_evs, gu_epi)
        asrcs = [actT[:, f, 0:Tn] for f in range(FC)]
        fevs = []

        def d_epi(i, r, bks, ev):
            a = E.emit("scalar", lambda e: e.activation(out=fT[:, 2 * i, 0:Tn], in_=bks[0][:, 0:Tn], func=AF.Copy), [ev], "act_b")
            d = E.emit("vector", lambda e: e.tensor_copy(out=fT[:, 2 * i + 1, 0:Tn], in_=bks[1][:, 0:Tn]), [ev], "dve_b")
            fevs[:] = [a, d]
            return [a, d]

        proj_run(E, FC, 8, asrcs, Tn, [E.last("dve_b")], d_epi)
        return [E.last("act_b"), E.last("dve_b")]

    def p1_tile(E, kind, xsrc, Tn, tile_idx, slot0, next_plan):
        ns = Tn // 128
        w = wr(E)
        st = E.__dict__.setdefault("p1", {"xinfree": [None, None], "xc": 0, "hbfree": None, "stg": None})
        hT_evs = []
        pre = st.setdefault("xpre", {})
        for s in range(ns):
            if s in pre:
                b, xe = pre.pop(s)
            else:
                b = st["xc"] % 2
                st["xc"] += 1
                xe = E.dma("gpsimd", xin[b][:, :], xsrc[s * 128:(s + 1) * 128, :], "xin%d" % b, [st["xinfree"][b]])
            evs, pel = tr_in(E, xin[b], xe, hT, s * 128)
            st["xinfree"][b] = pel
            hT_evs += evs
        fT_evs = ffn_run(E, Tn, hT_evs, st["stg"])
        wiv = wview(w_in)
        do_k = kind in ("own", "oth", "halo")
        do_q = kind == "own"
        do_u = kind in ("own", "halo")
        if do_u:
            proj_plan(E, wiv, KC, [3072 + i * 256 for i in range(4)])
        if do_k:
            proj_plan(E, wiv, KC, [1024 + i * 256 for i in range(4)])
        if do_q:
            proj_plan(E, wiv, KC, [i * 256 for i in range(4)])
        if do_k:
            proj_plan(E, wiv, KC, [2048 + i * 256 for i in range(4)])
        next_plan()
        h1T_evs = []

        def cons(s, ev, zt, zb):
            frees = []
            if kind == "own":
                r0 = tile_idx * T + s * 128
                frees.append(E.dma("sync", h1s[r0:r0 + 128, :], zt[:, :], "st_h%d" % zb, [ev]))
            c = E.emit("scalar", lambda e: e.activation(out=hb[:, :], in_=zt[:, :], func=AF.Copy), [ev, st["hbfree"]], "act_a")
            frees.append(c)

            def post():
                evs, pel = tr_in(E, hb, c, hT, s * 128)
                st["hbfree"] = pel
                h1T_evs.extend(evs)
            return frees, post

        xres = xsrc
        ln_epilogue(E, ns, fT_evs, lambda s: xres[s * 128:(s + 1) * 128, :], 0.5 / ALPHA, EPS_S, E.ln1_ev, cons)
        srcs = [hT[:, kc, 0:Tn] for kc in range(KC)]
        tok0 = tile_idx * T
        if kind == "own":
            kcol0 = (1 + tile_idx * 4) * 128
        elif kind == "oth":
            kcol0 = (1 + NQT + tile_idx * 4) * 128
        else:
            kcol0 = 0
        nvalid = NMETA if kind == "halo" else Tn
        stg_wait = st["stg"]
        if do_u:
            uev = []
            if kind == "own":
                uh = E.dma("sync", u_sb[:, :, :, 1:16], uhs[:, :, NMETA + tile_idx * 60:NMETA + (tile_idx + 1) * 60].rearrange("p c (s k) -> p c s k", s=4),
                           "ld_uh", [stg_wait, E.uh_done])

            def u_epi(i, r, bks, ev):
                out = []
                for oc in range(2):
                    ch = 2 * i + oc
                    if kind == "own":
                        dst = u_sb[:, ch, :, 16:144]
                        src = bks[oc][:, 0:Tn].rearrange("p (s c) -> p s c", s=4)
                    else:
                        dst = u_sb[:, ch, :, 0:128]
                        src = bks[oc][:, 0:Tn].rearrange("p (s c) -> p s c", s=4)
                    if oc == 0:
                        a = E.emit("scalar", lambda e: e.activation(out=dst, in_=src, func=AF.Copy), [ev, stg_wait], "act_b")
                    else:
                        a = E.emit("vector", lambda e: e.tensor_copy(out=dst, in_=src), [ev, stg_wait], "dve_b")
                    out.append(a)
                    uev.append(a)
                return out
            proj_run(E, KC, 4, srcs, Tn, h1T_evs, u_epi)
            if kind == "halo":
                for ch in range(8):
                    E.dma("sync", uhs[:, ch, :].rearrange("p (s c) -> p s c", s=4), u_sb[:, ch, :, 0:128], "st_u", [uev])
                E.uh_done = E.last("st_u")
            else:
                pool_p3 = pool_dve(E, uev + [uh])
        Tk = 128 if kind == "halo" else Tn
        nsk = Tk // 128
        srcs_k = [hT[:, kc, 0:Tk] for kc in range(KC)]
        if do_k:
            def k_epi(i, r, bks, ev):
                out = []
                for oc in range(2):
                    h = 2 * i + oc
                    if oc == 0:
                        a = E.emit("scalar", lambda e, h=h: e.activation(out=kst[:, h, 0:Tk], in_=bks[0][:, 0:Tk], func=AF.Copy), [ev, stg_wait], "act_b")
                    else:
                        a = E.emit("vector", lambda e, h=h: e.tensor_copy(out=kst[:, h, 0:Tk], in_=bks[1][:, 0:Tk]), [ev, stg_wait], "dve_b")
                    for m in range(2):
                        E.dma("sync", ks[h, m, 0:64, kcol0:kcol0 + nvalid], kst[m * 64:(m + 1) * 64, h, 0:nvalid], "st_k", [a])
                    out.append(a)
                return out
            proj_run(E, KC, 4, srcs_k, Tk, h1T_evs, k_epi)
        if do_q:
            qst = {"free": [None, None]}

            def q_epi(i, r, bks, ev):
                out = []
                for oc in range(2):
                    h = 2 * i + oc
                    qb = 2 * (i % 2) + oc
                    a = E.emit("scalar", lambda e, oc=oc, qb=qb: e.activation(out=qsb[qb][:, 0:Tn], in_=bks[oc][:, 0:Tn], func=AF.Copy, scale=0.125),
                               [ev, E.last("st_q%d" % qb)], "act_b")
                    for m in range(2):
                        E.dma("sync", qs[h, m, 0:64, tok0:tok0 + Tn], qsb[qb][m * 64:(m + 1) * 64, 0:Tn], "st_q%d" % qb, [a])
                    out.append(a)
                return out
            proj_run(E, KC, 4, srcs, Tn, h1T_evs, q_epi)
        if do_k:
            vst = {"free": [None, None]}
            ones_ev = E.emit("vector", lambda e: e.memset(v_stage[:, :, :, 128:129], 1.0), [stg_wait, fT_evs], "dve_b")

            def v_epi(i, r, bks, ev):
                out = []
                for oc in range(2):
                    h = 2 * i + oc
                    if oc == 0:
                        a = E.emit("scalar", lambda e: e.activation(out=vsb[0][:, 0:Tk], in_=bks[0][:, 0:Tk], func=AF.Copy), [ev, vst["free"][0]], "act_b")
                    else:
                        a = E.emit("vector", lambda e: e.tensor_copy(out=vsb[1][:, 0:Tk], in_=bks[1][:, 0:Tk]), [ev, vst["free"][1]], "dve_b")
                    out.append(a)
                    rr, tfree = next_tb(E)
                    tbv = banks[6 + rr][:, :].bitcast(BF16)
                    pe = None
                    for s in range(nsk):
                        pe = E.emit("tensor", lambda e, s=s, oc=oc: e.transpose(tbv[:, s * 128:(s + 1) * 128], vsb[oc][:, s * 128:(s + 1) * 128], ident_b[:, :]),
                                    [a, tfree] if s == 0 else [], "pe_tr" if s == nsk - 1 else None)
                    vst["free"][oc] = pe
                    dst = v_stage[:, h, 0:nsk, 0:128]
                    src = tbv[:, 0:nsk * 128].rearrange("p (s c) -> p s c", s=nsk)
                    c = E.emit("vector", lambda e: e.tensor_copy(out=dst, in_=src), [pe, stg_wait, ones_ev], "dve_b")
                    wr(E)["tbfree"][rr] = c
                    E.dma("sync", vs[h, :, kcol0 // 128:kcol0 // 128 + nsk, :] if kind != "halo" else vs[h, 0:NMETA, 0:1, :],
                          v_stage[:, h, 0:nsk, :] if kind != "halo" else v_stage[0:NMETA, h, 0:1, :], "st_v", [c])
                return out
            proj_run(E, KC, 4, srcs_k, Tk, h1T_evs, v_epi)
        if kind == "own":
            pool_mm(E, tile_idx, pool_p3)
        st["stg"] = [E.last("st_k"), E.last("st_v"), E.last("st_p"), E.last("st_u")]

    def pool_dve(E, uevs):
        d = lambda fn, waits: E.emit("vector", fn, waits, "dve_a")
        U = u_sb
        e1 = d(lambda e: e.tensor_tensor(out=wA[:, :, :, 0:143], in0=U[:, :, :, 1:144], in1=U[:, :, :, 0:143], op=OP.add), [uevs])
        e2 = d(lambda e: e.tensor_tensor(out=wB[:, :, :, 0:141], in0=wA[:, 2:8, :, 2:143], in1=wA[:, 2:8, :, 0:141], op=OP.add), [e1])
        p0 = d(lambda e: e.scalar_tensor_tensor(out=pooled[:, 0:2, :].rearrange("p c (s k) -> p c s k", s=4), in0=wA[:, 0:2, :, 15:143], scalar=0.5,
                                                in1=U[:, 0:2, :, 16:144], op0=OP.mult, op1=OP.subtract), [e2])
        p1 = d(lambda e: e.scalar_tensor_tensor(out=pooled[:, 2:4, :].rearrange("p c (s k) -> p c s k", s=4), in0=wB[:, 0:2, :, 13:141], scalar=0.25,
                                                in1=U[:, 2:4, :, 16:144], op0=OP.mult, op1=OP.subtract), [p0])
        e3 = d(lambda e: e.tensor_tensor(out=wA[:, 0:4, :, 0:137], in0=wB[:, 2:6, :, 4:141], in1=wB[:, 2:6, :, 0:137], op=OP.add), [p1])
        p2 = d(lambda e: e.scalar_tensor_tensor(out=pooled[:, 4:6, :].rearrange("p c (s k) -> p c s k", s=4), in0=wA[:, 0:2, :, 9:137], scalar=0.125,
                                                in1=U[:, 4:6, :, 16:144], op0=OP.mult, op1=OP.subtract), [e3])
        e4 = d(lambda e: e.tensor_tensor(out=wB[:, 0:2, :, 0:129], in0=wA[:, 2:4, :, 8:137], in1=wA[:, 2:4, :, 0:129], op=OP.add), [p2])
        p3 = d(lambda e: e.scalar_tensor_tensor(out=pooled[:, 6:8, :].rearrange("p c (s k) -> p c s k", s=4), in0=wB[:, 0:2, :, 1:129], scalar=0.0625,
                                                in1=U[:, 6:8, :, 16:144], op0=OP.mult, op1=OP.subtract), [e4])
        return p3

    def pool_mm(E, tile_idx, p3):
        for g in range(4):
            r, bfree = next_pair(E)
            bks = [banks[2 * r], banks[2 * r + 1]]
            ev = None
            for dc in range(2):
                for cc in range(2):
                    ev = mm(E, bks[dc][:, 0:T], wpool[:, (2 * g + cc) * 256 + dc * 128:(2 * g + cc) * 256 + (dc + 1) * 128], pooled[:, 2 * g + cc, :],
                            cc == 0, cc == 1, [p3, bfree] if (dc == 0 and cc == 0) else [], "pe_m" if (dc == 1 and cc == 1) else None)
            outs = []
            for dc in range(2):
                ch = 2 * g + dc
                a = E.emit("scalar", lambda e, ch=ch, dc=dc: e.activation(out=pout[:, ch, :], in_=bks[dc][:, 0:T], func=AF.Copy, scale=pscale[:, ch:ch + 1]), [ev], "act_b")
                outs.append(a)
            wr(E)["bpfree"][r] = outs
        E.dma("sync", ps[tile_idx], pout[:, :, :], "st_p", [E.last("act_b")])

    def phase1(E):
        E.ln1_ev = load_ln(E, "ln1_g", "ln1_b")
        E.uh_done = None
        tiles = [("halo", x_halo, T, 0)]
        tiles += [("own", x_own[i * T:(i + 1) * T, :], T, i) for i in range(NTILE)]
        tiles += [("oth", x_oth[i * T:(i + 1) * T, :], T, i) for i in range(NTILE)]
        if stage == 0:
            tiles = tiles[:3]
        ffn_plan(E, w1g, w1u, w1d)
        def mk_next(ti):
            def f():
                ffn_plan(E, w1g, w1u, w1d)
                _, nxs, nTn, _ = tiles[ti + 1]
                st = E.p1
                for s in range(min(2, nTn // 128)):
                    b = st["xc"] % 2
                    st["xc"] += 1
                    st["xpre"][s] = (b, E.dma("gpsimd", xin[b][:, :], nxs[s * 128:(s + 1) * 128, :], "xin%d" % b, [st["xinfree"][b]]))
            return f

        for ti, (kind, xs, Tn, idx) in enumerate(tiles):
            nxt = mk_next(ti) if ti + 1 < len(tiles) else (lambda: None)
            p1_tile(E, kind, xs, Tn, idx, 0, nxt)
        fin = [E.last(k) for k in ["st_h0", "st_h1", "st_h2", "st_q0", "st_q1", "st_q2", "st_q3", "st_k", "st_v", "st_p", "st_u"]]
        barrier(E, fin)

    def phase2(E):
        st = {"ptfree": [None] * 3, "pc": 0, "sfree": {}, "sc": 0, "ofree": {}, "kvfree": [None, None], "aofree": [None, None], "fc": 0}

        def kv_load(h):
            K, V, Q = Kb[h % 2], Vb[h % 2], Qb[h % 2]
            waits = [st["kvfree"][h % 2]]
            sem = "kv%d" % (h % 2)
            for m in range(2):
                E.dma("sync", K[0:67, m, :], ks[h, m, :, :], sem, waits)
                E.dma("sync", Q[0:67, m, :], qs[h, m, :, :], sem, waits)
            return E.dma("sync", V[:, :, :], vs[h], sem, waits)

        kv_evs = {0: kv_load(0)}
        for h in range(NH):
            K, V, Q = Kb[h % 2], Vb[h % 2], Qb[h % 2]
            kvev = kv_evs.pop(h)
            if h + 1 < NH:
                kv_evs[h + 1] = kv_load(h + 1)
            ao = aoh[h % 2]
            cdiag = corr[:, h * 128:(h + 1) * 128]
            steps = []
            for j in range(NQT // 2):
                n0, n1 = 2 * j, 2 * j + 1
                common = [0] + list(range(1, n0 + 2)) + list(range(1 + NQT, 1 + NQT + n0 + 1))
                for ci, sl in enumerate(common):
                    fix = cdiag if sl == n0 + 1 else (corr2[:, :] if sl == 1 + NQT + n0 else None)
                    qk = [(m * 256, 256, m, sl, n0 * 128, [(m * 256, fix)] if fix is not None else []) for m in range(2)]
                    av = [(m * 256 + qt * 128, qt, m, sl, ci == 0, (ci == len(common) - 1) and qt == 0) for qt in range(2) for m in range(2)]
                    steps.append({"j": j, "qk": qk, "av": av, "fin": [(0, n0)] if ci == len(common) - 1 else []})
                ex = [(n1 + 1, cdiag), (1 + NQT + n1, corr2[:, :])]
                qk = [((ti * 2 + m) * 128, 128, m, sl, n1 * 128, [((ti * 2 + m) * 128, fx)]) for ti, (sl, fx) in enumerate(ex) for m in range(2)]
                av = [((ti * 2 + m) * 128, 1, m, sl, False, ti == 1) for ti, (sl, fx) in enumerate(ex) for m in range(2)]
                steps.append({"j": j, "qk": qk, "av": av, "fin": [(1, n1)]})
            qk_ev = {}

            def emit_qk(g):
                sp = steps[g]
                sr = st["sc"] % 4
                st["sc"] += 1
                sbk = banks[sr]
                pe = None
                nq = len(sp["qk"])
                for bi, (c0, wd, m, sl, q0, fixes) in enumerate(sp["qk"]):
                    waits = [kvev, st["sfree"].get(sr)] if bi == 0 else []
                    lastb = (bi == nq - 1)
                    pe = mm(E, sbk[:, c0:c0 + wd], K[0:67, m, sl * 128:(sl + 1) * 128], Q[0:67, m, q0:q0 + wd],
                            True, not fixes, waits, "pe_m" if (lastb and not fixes) else None, skip=True)
                    for fi, (fc0, ftile) in enumerate(fixes):
                        pe = mm(E, sbk[:, fc0:fc0 + 128], ident_b[:, :], ftile, False, True, [],
                                "pe_m" if (lastb and fi == len(fixes) - 1) else None, skip=True)
                qk_ev[g] = (sr, pe)

            LA = 2
            for g in range(min(LA, len(steps))):
                emit_qk(g)
            pend = []
            fin_ev = [None]

            def flush_fin(min_age):
                keep = []
                for it in pend:
                    age, b, n_, c4 = it
                    if age >= min_age:
                        c5 = E.emit("scalar", lambda e: e.activation(out=rz[b][:, 4:5], in_=rz[b][:, 3:4], func=AF.Ln, bias=epst[:, 1:2], scale=1.0 / 128.0), [c4], "act_a")
                        c6 = E.emit("scalar", lambda e: e.activation(out=rz[b][:, 5:6], in_=rz[b][:, 4:5], func=AF.Exp, scale=-0.5), [c5], "act_a")
                        fin_ev[0] = E.emit("vector", lambda e: e.scalar_tensor_tensor(out=ao[:, n_, :], in0=o1[b][:, :], scalar=rz[b][:, 5:6], in1=gsub[:, :], op0=OP.mult, op1=OP.mult),
                                           [c6, st["aofree"][h % 2]], "dve_b")
                    else:
                        it[0] += 1
                        keep.append(it)
                pend[:] = keep

            for g in range(len(steps)):
                sp = steps[g]
                j = sp["j"]
                sr, pe = qk_ev.pop(g)
                sbk = banks[sr]
                ovs = [banks[4 + 2 * (j % 2) + qt][:, 0:258].rearrange("p (m c) -> p m c", m=2) for qt in range(2)]
                pr = st["pc"] % 3
                st["pc"] += 1
                pt = PT[pr]
                a = E.emit("scalar", lambda e: e.activation(out=pt[:, 0:512], in_=sbk[:, 0:512], func=AF.Exp, bias=kbias[:, h:h + 1], scale=1.0),
                           [pe, st["ptfree"][pr]], "act_b")
                st["sfree"][sr] = a
                pe2 = None
                na = len(sp["av"])
                lastev = {}
                for ai, (pc0, qt, m, sl, first, last) in enumerate(sp["av"]):
                    waits = [a] if ai == 0 else []
                    if first and m == 0:
                        waits = waits + [st["ofree"].get((j % 2, qt))]
                    pe2 = mm(E, ovs[qt][:, m, :], pt[:, pc0:pc0 + 128], V[:, sl, :], first and m == 0, last, waits,
                             "pe_m" if (ai == na - 1 or (last and m == 1)) else None, skip=True)
                    if m == 1:
                        lastev[qt] = pe2
                st["ptfree"][pr] = pe2
                if g + LA < len(steps):
                    emit_qk(g + LA)
                flush_fin(3)
                for (qt, n) in sp["fin"]:
                    ov = ovs[qt]
                    b = st["fc"] % 4
                    st["fc"] += 1
                    d = lambda fn, waits: E.emit("vector", fn, waits, "dve_b")
                    r1 = d(lambda e: e.reciprocal(out=rz[b][:, 0:2], in_=ov[:, :, 128]), [lastev[qt]])
                    r2 = d(lambda e: e.tensor_tensor(out=rz[b][:, 2:3], in0=rz[b][:, 1:2], in1=neglam[:, :], op=OP.mult), [r1])
                    c1 = d(lambda e: e.tensor_scalar(out=o1[b][:, :], in0=ov[:, 0, 0:128], scalar1=rz[b][:, 0:1], scalar2=None, op0=OP.mult), [r2])
                    c2 = d(lambda e: e.scalar_tensor_tensor(out=o1[b][:, :], in0=ov[:, 1, 0:128], scalar=rz[b][:, 2:3], in1=o1[b][:, :], op0=OP.mult, op1=OP.add), [c1])
                    st["ofree"][(j % 2, qt)] = c2
                    c3 = d(lambda e: e.tensor_tensor(out=o2[b][:, :], in0=o1[b][:, :], in1=o1[b][:, :], op=OP.mult), [c2])
                    c4 = d(lambda e: e.reduce_sum(out=rz[b][:, 3:4], in_=o2[b][:, :], axis=AX.X), [c3])
                    pend.append([0, b, n, c4])
            flush_fin(0)
            st["kvfree"][h % 2] = E.last("pe_m")
            st["aofree"][h % 2] = E.dma("sync", as_[:, :, h * 128:(h + 1) * 128].rearrange("n p c -> p n c"), ao[:, :, :], "st_a", [fin_ev[0]])
        barrier(E, [E.last("st_a")])

    def phase3(E):
        ln2_ev = load_ln(E, "ln2_g", "ln2_b")
        wov = wview(w_out)
        st = {"hbfree": None}
        for t in range(NTILE):
            tok0 = t * T
            proj_plan(E, wov, KC, [i * 256 for i in range(8)])
            ffn_plan(E, w2g, w2u, w2d)
            prev = [E.last("pe_m"), E.last("wfree")]
            la = E.dma("sync", a_tok[:, :, :], as_[t * 4:(t + 1) * 4].rearrange("n p c -> p n c"), "ld_a", prev)
            lp = E.dma("sync", mixT[:, 8:16, :], ps[t], "ld_p", prev)
            mevs = [lp]
            for s in range(4):
                evs, pel = tr_in(E, a_tok[:, s, :], la, mixT, s * 128, nchunks=8)
                mevs += evs
            srcs = [mixT[:, c, :] for c in range(KC)]
            fevs = []

            def o_epi(i, r, bks, ev):
                a = E.emit("scalar", lambda e: e.activation(out=fT[:, 2 * i, :], in_=bks[0][:, :], func=AF.Copy), [ev], "act_b")
                d = E.emit("vector", lambda e: e.tensor_copy(out=fT[:, 2 * i + 1, :], in_=bks[1][:, :]), [ev], "dve_b")
                return [a, d]
            proj_run(E, KC, 8, srcs, T, mevs, o_epi)
            fT_evs = [E.last("act_b"), E.last("dve_b")]
            if t == 0:
                E.ln2 = ln2_ev
            else:
                E.ln2 = load_ln(E, "ln2_g", "ln2_b", [E.last("dve_a")])
            h2T_evs = []
            h2st = []

            def cons2(s, ev, zt, zb):
                r0 = tok0 + s * 128
                f1 = E.dma("sync", h2s[r0:r0 + 128, :], zt[:, :], "st_h%d" % zb, [ev])
                h2st.append(f1)
                c = E.emit("scalar", lambda e: e.activation(out=hb[:, :], in_=zt[:, :], func=AF.Copy), [ev, st["hbfree"]], "act_a")

                def post():
                    evs, pel = tr_in(E, hb, c, hT, s * 128)
                    st["hbfree"] = pel
                    h2T_evs.extend(evs)
                return [f1, c], post
            ln_epilogue(E, 4, fT_evs, lambda s: h1s[tok0 + s * 128:tok0 + (s + 1) * 128, :], 1.0 / ALPHA, EPS_S, E.ln2, cons2)
            ln3_ev = load_ln(E, "ln3_g", "ln3_b", [E.last("dve_a")])
            fT_evs = ffn_run(E, T, h2T_evs)

            def cons3(s, ev, zt, zb):
                r0 = tok0 + s * 128
                return [E.dma("sync", y[r0:r0 + 128, :], zt[:, :], "st_y%d" % zb, [ev])]
            ln_epilogue(E, 4, fT_evs, lambda s: h2s[tok0 + s * 128:tok0 + (s + 1) * 128, :], 0.5 / ALPHA, EPS_S, [ln3_ev], cons3, res_waits=h2st)
        ys = [E.last("st_y0"), E.last("st_y1"), E.last("st_y2")]
        E.emit("sync", lambda e: e.nop(), ys)
        barrier(E, ys)

    def program(E):
        prologue(E)
        phase1(E)
        if stage >= 2:
            phase2(E)
        if stage >= 3:
            phase3(E)

    with nc.Block() as blk:
        for eng in ENGS:
            def run(e, eng=eng):
                P = passes[eng]
                P.e = e
                program(P)
            getattr(blk, eng)(run)
    es.close()
    return nc


def _consts():
    p = np.arange(128, dtype=np.float64)
    slopes = 2.0 ** (-(np.arange(NH) + 1.0))
    kbias = (p[:, None] * slopes[None, :]).astype(np.float32)
    corr = np.zeros((128, NH, 128), np.float32)
    k = np.arange(128)[:, None]
    q = np.arange(128)[None, :]
    masked = (k // 64) > (q // 64)
    later = (k > q) & ~masked
    for h in range(NH):
        c = np.where(later, -2.0 * slopes[h] * (k - q), 0.0)
        c = np.where(masked, NEG, c)
        corr[:, h, :] = c
    return kbias, corr.reshape(128, NH * 128), slopes


_NC_CACHE = {}


def kernel(x, meta_tokens, ln1_g, ln1_b, ffn1_w_gate, ffn1_w_up, ffn1_w_down, w_in, lambda_q1, lambda_k1,
           lambda_q2, lambda_k2, subln_g, w_pool, pool_scale, w_out, ln2_g, ln2_b, ffn2_w_gate, ffn2_w_up,
           ffn2_w_down, ln3_g, ln3_b, _stage=3, _debug=False):
    f = lambda a: np.ascontiguousarray(np.asarray(a, dtype=np.float32))
    x = f(x); meta_tokens = f(meta_tokens)
    kbias, corr, slopes = _consts()
    shared = {
        "ident": np.eye(128, dtype=np.float32), "kbias": kbias, "corr": corr,
        "pscale": f(np.asarray(pool_scale).reshape(8, 128).T),
        "ffn1_w_gate": f(ffn1_w_gate)[0], "ffn1_w_up": f(ffn1_w_up)[0], "ffn1_w_down": f(ffn1_w_down)[0],
        "ffn2_w_gate": f(ffn2_w_gate)[0], "ffn2_w_up": f(ffn2_w_up)[0], "ffn2_w_down": f(ffn2_w_down)[0],
        "w_in": f(w_in)[0], "w_out": f(w_out)[0], "w_pool": f(w_pool)[0],
        "ln1_g": f(ln1_g), "ln1_b": f(ln1_b), "ln2_g": f(ln2_g), "ln2_b": f(ln2_b), "ln3_g": f(ln3_g), "ln3_b": f(ln3_b),
        "lambda_q1": f(lambda_q1), "lambda_k1": f(lambda_k1), "lambda_q2": f(lambda_q2), "lambda_k2": f(lambda_k2),
        "subln_g": f(subln_g),
    }
    meta_pad = np.zeros((128, D), np.float32)
    meta_pad[:NMETA] = meta_tokens
    in_maps = []
    for c in range(8):
        b, p = c // 2, c % 2
        tiles = x[b].reshape(64, 128, D)
        own = np.ascontiguousarray(tiles[p::2]).reshape(NOWN, D)
        oth = np.ascontiguousarray(tiles[(1 - p)::2]).reshape(NOWN, D)
        halo = np.zeros((T, D), np.float32)
        halo[:NMETA] = meta_tokens
        for n in range(NQT):
            i = 2 * n + p
            halo[NMETA + 15 * n:NMETA + 15 * n + 15] = meta_tokens[1:16] if i == 0 else x[b, 128 * i - 15:128 * i]
        t = np.arange(NOWN)
        jq = 2 * (t // 128) + p
        qa = np.zeros((NH, 3, NOWN), np.float32)
        for h in range(NH):
            qa[h, 0] = -slopes[h] * ((t % 128) + 16)
            qa[h, 1] = -slopes[h] * 128.0 * jq
            qa[h, 2] = slopes[h]
        ka = np.zeros((3, NKTOK), np.float32)
        ka[0, :NMETA] = 1.0; ka[1, :NMETA] = 1.0; ka[2, :NMETA] = -16.0
        for n in range(NQT):
            for (base, j) in ((1 + n, 2 * n + p), (1 + NQT + n, 2 * n + 1 - p)):
                ka[0, base * 128:(base + 1) * 128] = 1.0
                ka[1, base * 128:(base + 1) * 128] = 1.0
                ka[2, base * 128:(base + 1) * 128] = 128.0 * j
        m = dict(shared)
        m.update({"x_own": own, "x_oth": oth, "x_halo": halo, "qaug": qa, "kaug": ka,
                  "corr2": np.full((128, 128), NEG * (1 - p), np.float32)})
        in_maps.append(m)
    key = (_stage, _debug)
    if key not in _NC_CACHE:
        _NC_CACHE[key] = build(_stage, _debug)
    nc = _NC_CACHE[key]
    res = run_bass_kernel_spmd(nc, in_maps, core_ids=list(range(8)))
    if _debug:
        return res
    out = np.zeros((4, SEQ, D), np.float32)
    for c in range(8):
        b, p = c // 2, c % 2
        out[b].reshape(64, 128, D)[p::2] = res.results[c]["y"].reshape(NQT, 128, D)
    return out
```

```python
import math
from contextlib import ExitStack

import numpy as np
import concourse.bass as bass
import concourse.mybir as mybir
from concourse.bass_utils import run_bass_kernel_spmd

F32 = mybir.dt.float32
BF16 = mybir.dt.bfloat16
AF = mybir.ActivationFunctionType
OP = mybir.AluOpType
AX = mybir.AxisListType

D = 2048
DFF = 5632
SEQ = 8192
NMETA = 16
NH = 8
T = 512
NOWN = 4096
NTILE = NOWN // T
NQT = NOWN // 128
NKT = 1 + 2 * NQT
NKTOK = NKT * 128
KC = D // 128
FC = DFF // 128
ALPHA = 2.0 ** 0.25
LN_EPS = 1e-5
EPS_S = LN_EPS / (ALPHA * ALPHA)
LAM_INIT = 0.8 - 0.6 * math.exp(-0.3 * 0)
NEG = -30000.0
NW = 4

ENGS = ["sync", "scalar", "vector", "gpsimd", "tensor"]
SEM_NAMES = (["wld%d" % i for i in range(NW)] +
             ["wfree", "pe_tr", "pe_m", "xin0", "xin1", "res0", "res1", "res2", "st_q0", "st_q1", "st_q2", "st_q3", "act_a", "dve_a", "act_b", "dve_b",
              "st_h0", "st_h1", "st_h2", "st_q", "st_k", "st_v", "st_p", "st_u", "st_a", "st_y0", "st_y1", "st_y2", "cst_g", "cst_s",
              "kv0", "kv1", "ld_a", "ld_p", "ld_uh", "ld_ln", "phase"])


class Pass:
    def __init__(self, name, sems):
        self.name = name
        self.e = None
        self.sems = sems
        self.cnt = {}
        self.waited = {}

    def emit(self, eng, fn, waits=(), sig=None, amt=1):
        ev = None
        if sig is not None:
            self.cnt[sig] = self.cnt.get(sig, 0) + amt
            ev = (sig, self.cnt[sig])
        if self.name == eng:
            self._wait(waits)
            ins = fn(self.e)
            if sig is not None:
                ins.then_inc(self.sems[sig], amt)
        return ev

    def _wait(self, waits):
        for w in waits:
            if w is None:
                continue
            if isinstance(w, list):
                self._wait(w)
                continue
            s, v = w
            if self.waited.get(s, 0) < v:
                self.e.wait_ge(self.sems[s], v)
                self.waited[s] = v

    def dma(self, q, out, in_, sig, waits=()):
        return self.emit(q, lambda e: e.dma_start(out=out, in_=in_), waits, sig, 16)

    def last(self, sig):
        return (sig, self.cnt.get(sig, 0)) if self.cnt.get(sig, 0) > 0 else None


def build(stage=3, debug=False):
    nc = bass.Bass("TRN2", target_bir_lowering=False)
    es = ExitStack()

    def din(name, shape):
        return nc.dram_tensor(name, list(shape), F32, kind="ExternalInput").ap()

    def dscr(name, shape, dt):
        kind = "ExternalOutput" if debug else "Internal"
        return nc.dram_tensor(name, list(shape), dt, kind=kind).ap()

    x_own = din("x_own", [NOWN, D]); x_oth = din("x_oth", [NOWN, D])
    x_halo = din("x_halo", [T, D])
    qaug = din("qaug", [NH, 3, NOWN]); kaug = din("kaug", [3, NKTOK])
    ident_d = din("ident", [128, 128]); kbias_d = din("kbias", [128, NH])
    corr_d = din("corr", [128, NH * 128]); corr2_d = din("corr2", [128, 128])
    pscale_d = din("pscale", [128, 8])
    ln_d = {k: din(k, [1, D]) for k in ["ln1_g", "ln1_b", "ln2_g", "ln2_b", "ln3_g", "ln3_b"]}
    w1g = din("ffn1_w_gate", [D, DFF]); w1u = din("ffn1_w_up", [D, DFF]); w1d = din("ffn1_w_down", [DFF, D])
    w2g = din("ffn2_w_gate", [D, DFF]); w2u = din("ffn2_w_up", [D, DFF]); w2d = din("ffn2_w_down", [DFF, D])
    w_in = din("w_in", [D, 4096]); w_out = din("w_out", [D, D]); w_pool = din("w_pool", [4, 256, 256])
    lam_d = {k: din(k, [1, 64]) for k in ["lambda_q1", "lambda_k1", "lambda_q2", "lambda_k2"]}
    subln_d = din("subln_g", [1, 128])
    y = nc.dram_tensor("y", [NOWN, D], F32, kind="ExternalOutput").ap()

    h1s = dscr("h1s", [NOWN, D], F32); h2s = dscr("h2s", [NOWN, D], F32)
    qs = dscr("qs", [NH, 2, 67, NOWN], BF16); ks = dscr("ks", [NH, 2, 67, NKTOK], BF16)
    vs = dscr("vs", [NH, 128, NKT, 129], BF16); ps = dscr("ps", [NTILE, 128, 8, T], BF16)
    uhs = dscr("uhs", [128, 8, T], F32); as_ = dscr("as_", [NQT, 128, NH * 128], BF16)

    def wview(w):
        return w.rearrange("(k p) f -> p k f", p=128)

    sems = {n: es.enter_context(nc.semaphore(n)) for n in SEM_NAMES}
    passes = {n: Pass(n, sems) for n in ENGS}

    def sb(name, shape, dt):
        return es.enter_context(nc.sbuf_tensor("s_" + name, list(shape), dt))

    ident_b = sb("ident_b", [128, 128], BF16); ident_f = sb("ident_f", [128, 128], F32)
    kbias = sb("kbias", [128, NH], F32); corr = sb("corr", [128, NH * 128], BF16)
    corr2 = sb("corr2", [128, 128], BF16); pscale = sb("pscale", [128, 8], F32)
    neglam = sb("neglam", [128, 1], F32); gsub = sb("gsub", [128, 128], F32)
    lamt = sb("lamt", [128, 256], F32); lamr = sb("lamr", [128, 4], F32)
    wpool = sb("wpool", [128, 8 * 256], BF16)
    lng = sb("lng", [128, D], F32); lnb = sb("lnb", [128, D], F32)
    zero_t = sb("zero_t", [128, 256], BF16)
    stats = sb("stats", [128, 72], F32); mv = sb("mv", [128, 24], F32); epst = sb("epst", [128, 2], F32)
    BIG = 180 * 1024
    big = sb("big", [128, BIG // 2], BF16)
    banks = [es.enter_context(nc.psum_tensor("bank%d" % i, [128, 512], F32)) for i in range(8)]

    off = {"o": 0}

    def carve(shape, dt, reset=None):
        if reset is not None:
            off["o"] = reset
        n = 1
        for s_ in shape[1:]:
            n *= s_
        nb = n * (2 if dt == BF16 else 4)
        o = off["o"]
        assert o % 4 == 0
        off["o"] = o + ((nb + 31) // 32) * 32
        assert off["o"] <= BIG, ("sbuf big overflow", off["o"])
        ap = big[0:shape[0], o // 2:o // 2 + nb // 2]
        if dt == F32:
            ap = ap.bitcast(F32)
        if len(shape) == 3:
            ap = ap.rearrange("p (a b) -> p a b", a=shape[1])
        elif len(shape) == 4:
            ap = ap.rearrange("p (a b c) -> p a b c", a=shape[1], b=shape[2])
        return ap

    hT = carve([128, KC, T], BF16, reset=0)
    actT = carve([128, FC, T], BF16)
    a0 = off["o"] - FC * T * 2
    wsl = [carve([128, 4096], BF16) for _ in range(NW)]
    fT = carve([128, KC, T], F32)
    f0 = off["o"] - KC * T * 4
    xin = [carve([128, D], BF16) for _ in range(2)]
    zts = [carve([128, D], F32) for _ in range(3)]
    hb = carve([128, D], BF16)
    sg = [carve([128, T], F32) for _ in range(2)]
    kst = carve([128, NH, T], BF16)
    qsb = [carve([128, T], BF16) for _ in range(4)]
    vsb = [carve([128, T], BF16) for _ in range(2)]
    ffn_end = off["o"]
    u_sb = carve([128, 8, 4, 144], F32, reset=a0)
    v_stage = carve([128, NH, 4, 129], BF16)
    pooled = carve([128, 8, T], BF16)
    pout = carve([128, 8, T], BF16)
    assert off["o"] <= a0 + FC * T * 2, off["o"] - a0
    mixT = carve([128, KC, T], BF16, reset=a0)
    a_tok = carve([128, 4, NH * 128], BF16)
    wA = carve([128, 8, 4, 144], F32, reset=f0)
    wB = carve([128, 6, 4, 144], F32)
    assert off["o"] <= f0 + KC * T * 4
    Kb = [carve([128, 2, NKTOK], BF16, reset=(0 if i == 0 else None)) for i in range(2)]
    Vb = [carve([128, NKT, 129], BF16) for _ in range(2)]
    Qb = [carve([128, 2, NOWN], BF16) for _ in range(2)]
    aoh = [carve([128, NQT, 128], BF16) for _ in range(2)]
    PT = [carve([128, 512], BF16) for _ in range(3)]
    o1 = [carve([128, 128], F32) for _ in range(4)]
    o2 = [carve([128, 128], F32) for _ in range(4)]
    rz = [carve([128, 8], F32) for _ in range(4)]
    att_end = off["o"]

    def barrier(E, extra=()):
        n = E.cnt.get("phase", 0)
        tgt = n + len(ENGS)
        for eng in ENGS:
            E.emit(eng, lambda e: e.nop(), extra, "phase")
        for eng in ENGS:
            E.emit(eng, lambda e: e.nop(), [("phase", tgt)])

    def prologue(E):
        E.dma("gpsimd", ident_b[:, :], ident_d, "cst_g")
        E.dma("sync", ident_f[:, :], ident_d, "cst_s")
        E.dma("sync", kbias[:, :], kbias_d, "cst_s")
        E.dma("gpsimd", corr[:, :], corr_d, "cst_g")
        E.dma("gpsimd", corr2[:, :], corr2_d, "cst_g")
        E.dma("sync", pscale[:, :], pscale_d, "cst_s")
        E.dma("gpsimd", wpool[:, :].rearrange("p (a d) -> p a d", a=8),
              w_pool.rearrange("g (c p) d -> p (g c) d", p=128), "cst_g")
        for i, k in enumerate(["lambda_q1", "lambda_k1", "lambda_q2", "lambda_k2"]):
            E.dma("sync", lamt[:, i * 64:(i + 1) * 64], lam_d[k].broadcast_to([128, 64]), "cst_s")
        cst = E.dma("sync", gsub[:, :], subln_d.broadcast_to([128, 128]), "cst_s")
        E.emit("vector", lambda e: e.memset(epst[:, 0:1], EPS_S), (), "dve_a")
        E.emit("vector", lambda e: e.memset(epst[:, 1:2], LN_EPS), (), "dve_a")
        e0 = E.emit("vector", lambda e: e.memset(zero_t[:, :], 0.0), (), "dve_a")
        for h in range(NH):
            for m in range(2):
                E.dma("sync", ks[h, m, :, 0:128], zero_t[0:67, 0:128], "st_k", [e0])
        kz = E.last("st_k")
        for h in range(NH):
            E.dma("sync", vs[h, :, 0, :], zero_t[:, 0:129], "st_v", [e0])
        for h in range(NH):
            for m in range(2):
                E.dma("gpsimd", ks[h, m, 64:67, :], kaug, "cst_g", [kz])
                E.dma("gpsimd", qs[h, m, 64:67, :], qaug[h], "cst_g")
        a1 = E.emit("vector", lambda e: e.tensor_tensor(out=lamt[:, 0:64], in0=lamt[:, 0:64], in1=lamt[:, 64:128], op=OP.mult), [cst], "dve_a")
        a2 = E.emit("vector", lambda e: e.tensor_tensor(out=lamt[:, 128:192], in0=lamt[:, 128:192], in1=lamt[:, 192:256], op=OP.mult), [a1], "dve_a")
        a3 = E.emit("vector", lambda e: e.reduce_sum(out=lamr[:, 0:1], in_=lamt[:, 0:64], axis=AX.X), [a2], "dve_a")
        a4 = E.emit("vector", lambda e: e.reduce_sum(out=lamr[:, 1:2], in_=lamt[:, 128:192], axis=AX.X), [a3], "dve_a")
        b1 = E.emit("scalar", lambda e: e.activation(out=lamr[:, 2:4], in_=lamr[:, 0:2], func=AF.Exp), [a4], "act_a")
        a5 = E.emit("vector", lambda e: e.tensor_tensor(out=neglam[:, :], in0=lamr[:, 3:4], in1=lamr[:, 2:3], op=OP.subtract), [b1], "dve_a")
        a6 = E.emit("vector", lambda e: e.tensor_scalar(out=neglam[:, :], in0=neglam[:, :], scalar1=-LAM_INIT, scalar2=None, op0=OP.add), [a5], "dve_a")
        E.emit("vector", lambda e: e.tensor_scalar(out=gsub[:, :], in0=gsub[:, :], scalar1=1.0 - LAM_INIT, scalar2=None, op0=OP.mult), [a6], "dve_a")
        barrier(E, [E.last("cst_g"), E.last("cst_s")])

    def wr(E):
        if not hasattr(E, "wr_"):
            E.wr_ = {"plan": [], "issued": 0, "taken": 0, "evs": {}, "bp": 0, "bpfree": {}, "tb": 0, "tbfree": {}}
        return E.wr_

    def wpump(E):
        w = wr(E)
        while w["issued"] < len(w["plan"]) and w["issued"] - w["taken"] < NW:
            k = w["issued"]
            src, nk = w["plan"][k]
            slot = k % NW
            waits = [("wfree", k - NW + 1)] if k >= NW else []
            dst = wsl[slot][:, 0:nk * 256].rearrange("p (k c) -> p k c", k=nk)
            w["evs"][k] = E.dma("gpsimd", dst, src, "wld%d" % slot, waits)
            w["issued"] += 1

    def proj_plan(E, Wv, nk, cols):
        for c0 in cols:
            k0 = 0
            while k0 < nk:
                n = min(16, nk - k0)
                wr(E)["plan"].append((Wv[:, k0:k0 + n, c0:c0 + 256], n))
                k0 += n

    def next_pair(E):
        w = wr(E)
        r = w["bp"] % 3
        w["bp"] += 1
        return r, w["bpfree"].get(r)

    def next_tb(E):
        w = wr(E)
        r = w["tb"] % 2
        w["tb"] += 1
        return r, w["tbfree"].get(r)

    def mm(E, out, lhsT, rhs, start, stop, waits=(), sig=None, skip=False):
        return E.emit("tensor", lambda e: e.matmul(out, lhsT, rhs, start=start, stop=stop, skip_group_check=skip), waits, sig)

    def proj_run(E, nk, n_ocp, srcs, Tn, src_waits, epilogue):
        w = wr(E)
        for i in range(n_ocp):
            r, bfree = next_pair(E)
            bks = [banks[2 * r], banks[2 * r + 1]]
            k0 = 0
            ev = None
            while k0 < nk:
                n = min(16, nk - k0)
                wpump(E)
                k = w["taken"]
                assert k < w["issued"]
                w["taken"] += 1
                slot = k % NW
                wev = w["evs"].pop(k)
                wt = wsl[slot][:, 0:n * 256].rearrange("p (k c) -> p k c", k=n)
                for oc in range(2):
                    for kk in range(n):
                        kc = k0 + kk
                        waits = []
                        if kk == 0 and oc == 0:
                            waits = [wev]
                            if kc == 0:
                                waits += [bfree, list(src_waits)]
                        lastmm = (oc == 1 and kk == n - 1)
                        ev_ = mm(E, bks[oc][:, 0:Tn], wt[:, kk, oc * 128:(oc + 1) * 128], srcs[kc],
                                 kc == 0, kc == nk - 1, waits, "wfree" if lastmm else None)
                        if lastmm:
                            ev = ev_
                k0 += n
                wpump(E)
            w["bpfree"][r] = epilogue(i, r, bks, ev)

    def tr_in(E, src_tok, src_ev, dst, col0, nchunks=KC):
        w = wr(E)
        evs = []
        pe_last = None
        for g in range(nchunks // 8):
            r, tfree = next_tb(E)
            tbv = banks[6 + r][:, :].bitcast(BF16)
            for i in range(8):
                c = g * 8 + i
                waits = [src_ev, tfree] if i == 0 else []
                pe_last = E.emit("tensor", lambda e, c=c, i=i: e.transpose(tbv[:, i * 128:(i + 1) * 128], src_tok[:, c * 128:(c + 1) * 128], ident_b[:, :]),
                                 waits, "pe_tr" if i == 7 else None)
            eng, sgn = (("vector", "dve_a") if g % 2 == 0 else ("scalar", "act_a"))
            src3 = tbv.rearrange("p (k c) -> p k c", k=8)
            dst3 = dst[:, g * 8:(g + 1) * 8, col0:col0 + 128]
            if eng == "vector":
                ev = E.emit("vector", lambda e: e.tensor_copy(out=dst3, in_=src3), [pe_last], sgn)
            else:
                ev = E.emit("scalar", lambda e: e.activation(out=dst3, in_=src3, func=AF.Copy), [pe_last], sgn)
            w["tbfree"][r] = ev
            evs.append(ev)
        return evs, pe_last

    def ln_epilogue(E, ns, fT_evs, resid_src, cscale, eps, ln_ev, consumer, res_waits=None):
        w = wr(E)
        st = E.__dict__.setdefault("ln_st", {"free": [None, None, None], "zc": 0})
        info = {}
        loads = {}

        def load(s):
            zb = st["zc"] % 3
            st["zc"] += 1
            loads[s] = (zb, E.dma("sync", zts[zb][:, :], resid_src(s), "res%d" % zb, [st["free"][zb], res_waits]))

        def stage_a(s):
            zb, rev = loads.pop(s)
            z = zts[zb]
            sts = stats[:, zb * 24:(zb + 1) * 24]
            m_ = mv[:, zb * 8:(zb + 1) * 8]
            zevs = []
            for qd in range(4):
                r, tfree = next_tb(E)
                tbk = banks[6 + r]
                pe = None
                for i in range(4):
                    waits = [list(fT_evs), tfree] if i == 0 else []
                    pe = E.emit("tensor", lambda e, i=i, qd=qd: e.transpose(tbk[:, i * 128:(i + 1) * 128], fT[:, 4 * qd + i, s * 128:(s + 1) * 128], ident_f[:, :]),
                                waits, "pe_tr" if i == 3 else None)
                zsl = z[:, qd * 512:(qd + 1) * 512]
                ev = E.emit("vector", lambda e: e.scalar_tensor_tensor(out=zsl, in0=tbk[:, :], scalar=cscale, in1=zsl, op0=OP.mult, op1=OP.add),
                            [pe, rev], "dve_a")
                w["tbfree"][r] = ev
                ev2 = E.emit("vector", lambda e, qd=qd: e.bn_stats(out=sts[:, qd * 6:(qd + 1) * 6], in_=zsl), [ev], "dve_a")
                zevs.append(ev2)
            e1 = E.emit("vector", lambda e: e.bn_aggr(out=m_[:, 0:2], in_=sts), [zevs[-1]], "dve_a")
            e2a = E.emit("scalar", lambda e: e.activation(out=m_[:, 4:5], in_=m_[:, 1:2], func=AF.Ln, bias=epst[:, 0:1], scale=1.0), [e1], "act_a")
            e2 = E.emit("scalar", lambda e: e.activation(out=m_[:, 2:3], in_=m_[:, 4:5], func=AF.Exp, scale=-0.5), [e2a], "act_a")
            e3 = E.emit("vector", lambda e: e.scalar_tensor_tensor(out=m_[:, 3:4], in0=m_[:, 0:1], scalar=-1.0, in1=m_[:, 2:3], op0=OP.mult, op1=OP.mult), [e2], "dve_a")
            e4 = E.emit("scalar", lambda e: e.activation(out=z[:, :], in_=z[:, :], func=AF.Identity, bias=m_[:, 3:4], scale=m_[:, 2:3]), [e3], "act_a")
            info[s] = (zb, z, e4)

        def stage_c(s):
            zb, z, e4 = info.pop(s)
            e5 = E.emit("vector", lambda e: e.tensor_tensor(out=z[:, :], in0=z[:, :], in1=lng[:, :], op=OP.mult), [e4, ln_ev], "dve_a")
            e6 = E.emit("vector", lambda e: e.tensor_tensor(out=z[:, :], in0=z[:, :], in1=lnb[:, :], op=OP.add), [e5], "dve_a")
            st["free"][zb] = consumer(s, e6, z, zb)

        for s in range(min(2, ns)):
            load(s)
        stage_a(0)
        for s in range(ns):
            if s + 2 < ns:
                load(s + 2)
            if s + 1 < ns:
                stage_a(s + 1)
            stage_c(s)

    def load_ln(E, gk, bk, waits=()):
        E.dma("sync", lng[:, :], ln_d[gk].broadcast_to([128, D]), "ld_ln", waits)
        return E.dma("sync", lnb[:, :], ln_d[bk].broadcast_to([128, D]), "ld_ln", waits)

    def ffn_plan(E, wg, wu, wd):
        wgv, wuv, wdv = wview(wg), wview(wu), wview(wd)
        for j in range(FC // 2):
            proj_plan(E, wgv, KC, [j * 256])
            proj_plan(E, wuv, KC, [j * 256])
        proj_plan(E, wdv, FC, [i * 256 for i in range(8)])

    def ffn_run(E, Tn, hT_evs, act_waits=None):
        st = {"g": None, "sgfree": [None, None], "mul": []}
        srcs = [hT[:, kc, 0:Tn] for kc in range(KC)]

        def gu_epi(i, r, bks, ev):
            if i % 2 == 0:
                st["g"] = (r, bks, ev)
                return wr(E)["bpfree"].get(r)
            j = i // 2
            rg, gb, gev = st["g"]
            afree, dfree = [], []
            for fc in range(2):
                a = E.emit("scalar", lambda e, fc=fc: e.activation(out=sg[fc][:, 0:Tn], in_=gb[fc][:, 0:Tn], func=AF.Silu),
                           [gev, st["sgfree"][fc]], "act_b")
                d = E.emit("vector", lambda e, fc=fc: e.tensor_tensor(out=actT[:, 2 * j + fc, 0:Tn], in0=sg[fc][:, 0:Tn], in1=bks[fc][:, 0:Tn], op=OP.mult),
                           [a, ev, act_waits], "dve_b")
                st["sgfree"][fc] = d
                afree.append(a); dfree.append(d)
            wr(E)["bpfree"][rg] = afree
            st["mul"] = dfree
            return dfree

        proj_run(E, KC, FC, srcs, Tn, hT_evs, gu_epi)
        asrcs = [actT[:, f, 0:Tn] for f in range(FC)]
        fevs = []

        def d_epi(i, r, bks, ev):
            a = E.emit("scalar", lambda e: e.activation(out=fT[:, 2 * i, 0:Tn], in_=bks[0][:, 0:Tn], func=AF.Copy), [ev], "act_b")
            d = E.emit("vector", lambda e: e.tensor_copy(out=fT[:, 2 * i + 1, 0:Tn], in_=bks[1][:, 0:Tn]), [ev], "dve_b")
            fevs[:] = [a, d]
            return [a, d]

        proj_run(E, FC, 8, asrcs, Tn, [E.last("dve_b")], d_epi)
        return [E.last("act_b"), E.last("dve_b")]

    def p1_tile(E, kind, xsrc, Tn, tile_idx, slot0, next_plan):
        ns = Tn // 128
        w = wr(E)
        st = E.__dict__.setdefault("p1", {"xinfree": [None, None], "xc": 0, "hbfree": None, "stg": None})
        hT_evs = []
        pre = st.setdefault("xpre", {})
        for s in range(ns):
            if s in pre:
                b, xe = pre.pop(s)
            else:
                b = st["xc"] % 2
                st["xc"] += 1
                xe = E.dma("gpsimd", xin[b][:, :], xsrc[s * 128:(s + 1) * 128, :], "xin%d" % b, [st["xinfree"][b]])
            evs, pel = tr_in(E, xin[b], xe, hT, s * 128)
            st["xinfree"][b] = pel
            hT_evs += evs
        fT_evs = ffn_run(E, Tn, hT_evs, st["stg"])
        wiv = wview(w_in)
        do_k = kind in ("own", "oth", "halo")
        do_q = kind == "own"
        do_u = kind in ("own", "halo")
        if do_u:
            proj_plan(E, wiv, KC, [3072 + i * 256 for i in range(4)])
        if do_k:
            proj_plan(E, wiv, KC, [1024 + i * 256 for i in range(4)])
        if do_q:
            proj_plan(E, wiv, KC, [i * 256 for i in range(4)])
        if do_k:
            proj_plan(E, wiv, KC, [2048 + i * 256 for i in range(4)])
        next_plan()
        h1T_evs = []

        def cons(s, ev, zt, zb):
            frees = []
            if kind == "own":
                r0 = tile_idx * T + s * 128
                frees.append(E.dma("sync", h1s[r0:r0 + 128, :], zt[:, :], "st_h%d" % zb, [ev]))
            c = E.emit("scalar", lambda e: e.activation(out=hb[:, :], in_=zt[:, :], func=AF.Copy), [ev, st["hbfree"]], "act_a")
            frees.append(c)
            evs, pel = tr_in(E, hb, c, hT, s * 128)
            st["hbfree"] = pel
            h1T_evs.extend(evs)
            return frees

        xres = xsrc
        ln_epilogue(E, ns, fT_evs, lambda s: xres[s * 128:(s + 1) * 128, :], 0.5 / ALPHA, EPS_S, E.ln1_ev, cons)
        srcs = [hT[:, kc, 0:Tn] for kc in range(KC)]
        tok0 = tile_idx * T
        if kind == "own":
            kcol0 = (1 + tile_idx * 4) * 128
        elif kind == "oth":
            kcol0 = (1 + NQT + tile_idx * 4) * 128
        else:
            kcol0 = 0
        nvalid = NMETA if kind == "halo" else Tn
        stg_wait = st["stg"]
        if do_u:
            uev = []
            if kind == "own":
                uh = E.dma("sync", u_sb[:, :, :, 1:16], uhs[:, :, NMETA + tile_idx * 60:NMETA + (tile_idx + 1) * 60].rearrange("p c (s k) -> p c s k", s=4),
                           "ld_uh", [stg_wait, E.uh_done])

            def u_epi(i, r, bks, ev):
                out = []
                for oc in range(2):
                    ch = 2 * i + oc
                    if kind == "own":
                        dst = u_sb[:, ch, :, 16:144]
                        src = bks[oc][:, 0:Tn].rearrange("p (s c) -> p s c", s=4)
                    else:
                        dst = u_sb[:, ch, :, 0:128]
                        src = bks[oc][:, 0:Tn].rearrange("p (s c) -> p s c", s=4)
                    if oc == 0:
                        a = E.emit("scalar", lambda e: e.activation(out=dst, in_=src, func=AF.Copy), [ev, stg_wait], "act_b")
                    else:
                        a = E.emit("vector", lambda e: e.tensor_copy(out=dst, in_=src), [ev, stg_wait], "dve_b")
                    out.append(a)
                    uev.append(a)
                return out
            proj_run(E, KC, 4, srcs, Tn, h1T_evs, u_epi)
            if kind == "halo":
                for ch in range(8):
                    E.dma("sync", uhs[:, ch, :].rearrange("p (s c) -> p s c", s=4), u_sb[:, ch, :, 0:128], "st_u", [uev])
                E.uh_done = E.last("st_u")
            else:
                pool_p3 = pool_dve(E, uev + [uh])
        Tk = 128 if kind == "halo" else Tn
        nsk = Tk // 128
        srcs_k = [hT[:, kc, 0:Tk] for kc in range(KC)]
        if do_k:
            def k_epi(i, r, bks, ev):
                out = []
                for oc in range(2):
                    h = 2 * i + oc
                    if oc == 0:
                        a = E.emit("scalar", lambda e, h=h: e.activation(out=kst[:, h, 0:Tk], in_=bks[0][:, 0:Tk], func=AF.Copy), [ev, stg_wait], "act_b")
                    else:
                        a = E.emit("vector", lambda e, h=h: e.tensor_copy(out=kst[:, h, 0:Tk], in_=bks[1][:, 0:Tk]), [ev, stg_wait], "dve_b")
                    for m in range(2):
                        E.dma("sync", ks[h, m, 0:64, kcol0:kcol0 + nvalid], kst[m * 64:(m + 1) * 64, h, 0:nvalid], "st_k", [a])
                    out.append(a)
                return out
            proj_run(E, KC, 4, srcs_k, Tk, h1T_evs, k_epi)
        if do_q:
            qst = {"free": [None, None]}

            def q_epi(i, r, bks, ev):
                out = []
                for oc in range(2):
                    h = 2 * i + oc
                    qb = 2 * (i % 2) + oc
                    a = E.emit("scalar", lambda e, oc=oc, qb=qb: e.activation(out=qsb[qb][:, 0:Tn], in_=bks[oc][:, 0:Tn], func=AF.Copy, scale=0.125),
                               [ev, E.last("st_q%d" % qb)], "act_b")
                    for m in range(2):
                        E.dma("sync", qs[h, m, 0:64, tok0:tok0 + Tn], qsb[qb][m * 64:(m + 1) * 64, 0:Tn], "st_q%d" % qb, [a])
                    out.append(a)
                return out
            proj_run(E, KC, 4, srcs, Tn, h1T_evs, q_epi)
        if do_k:
            vst = {"free": [None, None]}
            ones_ev = E.emit("vector", lambda e: e.memset(v_stage[:, :, :, 128:129], 1.0), [stg_wait, fT_evs], "dve_b")

            def v_epi(i, r, bks, ev):
                out = []
                for oc in range(2):
                    h = 2 * i + oc
                    if oc == 0:
                        a = E.emit("scalar", lambda e: e.activation(out=vsb[0][:, 0:Tk], in_=bks[0][:, 0:Tk], func=AF.Copy), [ev, vst["free"][0]], "act_b")
                    else:
                        a = E.emit("vector", lambda e: e.tensor_copy(out=vsb[1][:, 0:Tk], in_=bks[1][:, 0:Tk]), [ev, vst["free"][1]], "dve_b")
                    out.append(a)
                    rr, tfree = next_tb(E)
                    tbv = banks[6 + rr][:, :].bitcast(BF16)
                    pe = None
                    for s in range(nsk):
                        pe = E.emit("tensor", lambda e, s=s, oc=oc: e.transpose(tbv[:, s * 128:(s + 1) * 128], vsb[oc][:, s * 128:(s + 1) * 128], ident_b[:, :]),
                                    [a, tfree] if s == 0 else [], "pe_tr" if s == nsk - 1 else None)
                    vst["free"][oc] = pe
                    dst = v_stage[:, h, 0:nsk, 0:128]
                    src = tbv[:, 0:nsk * 128].rearrange("p (s c) -> p s c", s=nsk)
                    c = E.emit("vector", lambda e: e.tensor_copy(out=dst, in_=src), [pe, stg_wait, ones_ev], "dve_b")
                    wr(E)["tbfree"][rr] = c
                    E.dma("sync", vs[h, :, kcol0 // 128:kcol0 // 128 + nsk, :] if kind != "halo" else vs[h, 0:NMETA, 0:1, :],
                          v_stage[:, h, 0:nsk, :] if kind != "halo" else v_stage[0:NMETA, h, 0:1, :], "st_v", [c])
                return out
            proj_run(E, KC, 4, srcs_k, Tk, h1T_evs, v_epi)
        if kind == "own":
            pool_mm(E, tile_idx, pool_p3)
        st["stg"] = [E.last("st_k"), E.last("st_v"), E.last("st_p"), E.last("st_u")]

    def pool_dve(E, uevs):
        d = lambda fn, waits: E.emit("vector", fn, waits, "dve_a")
        U = u_sb
        e1 = d(lambda e: e.tensor_tensor(out=wA[:, :, :, 0:143], in0=U[:, :, :, 1:144], in1=U[:, :, :, 0:143], op=OP.add), [uevs])
        e2 = d(lambda e: e.tensor_tensor(out=wB[:, :, :, 0:141], in0=wA[:, 2:8, :, 2:143], in1=wA[:, 2:8, :, 0:141], op=OP.add), [e1])
        p0 = d(lambda e: e.scalar_tensor_tensor(out=pooled[:, 0:2, :].rearrange("p c (s k) -> p c s k", s=4), in0=wA[:, 0:2, :, 15:143], scalar=0.5,
                                                in1=U[:, 0:2, :, 16:144], op0=OP.mult, op1=OP.subtract), [e2])
        p1 = d(lambda e: e.scalar_tensor_tensor(out=pooled[:, 2:4, :].rearrange("p c (s k) -> p c s k", s=4), in0=wB[:, 0:2, :, 13:141], scalar=0.25,
                                                in1=U[:, 2:4, :, 16:144], op0=OP.mult, op1=OP.subtract), [p0])
        e3 = d(lambda e: e.tensor_tensor(out=wA[:, 0:4, :, 0:137], in0=wB[:, 2:6, :, 4:141], in1=wB[:, 2:6, :, 0:137], op=OP.add), [p1])
        p2 = d(lambda e: e.scalar_tensor_tensor(out=pooled[:, 4:6, :].rearrange("p c (s k) -> p c s k", s=4), in0=wA[:, 0:2, :, 9:137], scalar=0.125,
                                                in1=U[:, 4:6, :, 16:144], op0=OP.mult, op1=OP.subtract), [e3])
        e4 = d(lambda e: e.tensor_tensor(out=wB[:, 0:2, :, 0:129], in0=wA[:, 2:4, :, 8:137], in1=wA[:, 2:4, :, 0:129], op=OP.add), [p2])
        p3 = d(lambda e: e.scalar_tensor_tensor(out=pooled[:, 6:8, :].rearrange("p c (s k) -> p c s k", s=4), in0=wB[:, 0:2, :, 1:129], scalar=0.0625,
                                                in1=U[:, 6:8, :, 16:144], op0=OP.mult, op1=OP.subtract), [e4])
        return p3

    def pool_mm(E, tile_idx, p3):
        for g in range(4):
            r, bfree = next_pair(E)
            bks = [banks[2 * r], banks[2 * r + 1]]
            ev = None
            for dc in range(2):
                for cc in range(2):
                    ev = mm(E, bks[dc][:, 0:T], wpool[:, (2 * g + cc) * 256 + dc * 128:(2 * g + cc) * 256 + (dc + 1) * 128], pooled[:, 2 * g + cc, :],
                            cc == 0, cc == 1, [p3, bfree] if (dc == 0 and cc == 0) else [], "pe_m" if (dc == 1 and cc == 1) else None)
            outs = []
            for dc in range(2):
                ch = 2 * g + dc
                a = E.emit("scalar", lambda e, ch=ch, dc=dc: e.activation(out=pout[:, ch, :], in_=bks[dc][:, 0:T], func=AF.Copy, scale=pscale[:, ch:ch + 1]), [ev], "act_b")
                outs.append(a)
            wr(E)["bpfree"][r] = outs
        E.dma("sync", ps[tile_idx], pout[:, :, :], "st_p", [E.last("act_b")])

    def phase1(E):
        E.ln1_ev = load_ln(E, "ln1_g", "ln1_b")
        E.uh_done = None
        tiles = [("halo", x_halo, T, 0)]
        tiles += [("own", x_own[i * T:(i + 1) * T, :], T, i) for i in range(NTILE)]
        tiles += [("oth", x_oth[i * T:(i + 1) * T, :], T, i) for i in range(NTILE)]
        if stage == 0:
            tiles = tiles[:3]
        ffn_plan(E, w1g, w1u, w1d)
        def mk_next(ti):
            def f():
                ffn_plan(E, w1g, w1u, w1d)
                _, nxs, nTn, _ = tiles[ti + 1]
                st = E.p1
                for s in range(min(2, nTn // 128)):
                    b = st["xc"] % 2
                    st["xc"] += 1
                    st["xpre"][s] = (b, E.dma("gpsimd", xin[b][:, :], nxs[s * 128:(s + 1) * 128, :], "xin%d" % b, [st["xinfree"][b]]))
            return f

        for ti, (kind, xs, Tn, idx) in enumerate(tiles):
            nxt = mk_next(ti) if ti + 1 < len(tiles) else (lambda: None)
            p1_tile(E, kind, xs, Tn, idx, 0, nxt)
        fin = [E.last(k) for k in ["st_h0", "st_h1", "st_h2", "st_q0", "st_q1", "st_q2", "st_q3", "st_k", "st_v", "st_p", "st_u"]]
        barrier(E, fin)

    def phase2(E):
        st = {"ptfree": [None] * 3, "pc": 0, "sfree": {}, "sc": 0, "ofree": {}, "kvfree": [None, None], "aofree": [None, None], "fc": 0}

        def kv_load(h):
            K, V, Q = Kb[h % 2], Vb[h % 2], Qb[h % 2]
            waits = [st["kvfree"][h % 2]]
            sem = "kv%d" % (h % 2)
            for m in range(2):
                E.dma("sync", K[0:67, m, :], ks[h, m, :, :], sem, waits)
                E.dma("sync", Q[0:67, m, :], qs[h, m, :, :], sem, waits)
            return E.dma("sync", V[:, :, :], vs[h], sem, waits)

        kv_evs = {0: kv_load(0)}
        for h in range(NH):
            K, V, Q = Kb[h % 2], Vb[h % 2], Qb[h % 2]
            kvev = kv_evs.pop(h)
            if h + 1 < NH:
                kv_evs[h + 1] = kv_load(h + 1)
            ao = aoh[h % 2]
            cdiag = corr[:, h * 128:(h + 1) * 128]
            steps = []
            for j in range(NQT // 2):
                n0, n1 = 2 * j, 2 * j + 1
                common = [0] + list(range(1, n0 + 2)) + list(range(1 + NQT, 1 + NQT + n0 + 1))
                for ci, sl in enumerate(common):
                    fix = cdiag if sl == n0 + 1 else (corr2[:, :] if sl == 1 + NQT + n0 else None)
                    qk = [(m * 256, 256, m, sl, n0 * 128, [(m * 256, fix)] if fix is not None else []) for m in range(2)]
                    av = [(m * 256 + qt * 128, qt, m, sl, ci == 0, (ci == len(common) - 1) and qt == 0) for qt in range(2) for m in range(2)]
                    steps.append({"j": j, "qk": qk, "av": av, "fin": [(0, n0)] if ci == len(common) - 1 else []})
                ex = [(n1 + 1, cdiag), (1 + NQT + n1, corr2[:, :])]
                qk = [((ti * 2 + m) * 128, 128, m, sl, n1 * 128, [((ti * 2 + m) * 128, fx)]) for ti, (sl, fx) in enumerate(ex) for m in range(2)]
                av = [((ti * 2 + m) * 128, 1, m, sl, False, ti == 1) for ti, (sl, fx) in enumerate(ex) for m in range(2)]
                steps.append({"j": j, "qk": qk, "av": av, "fin": [(1, n1)]})
            qk_ev = {}

            def emit_qk(g):
                sp = steps[g]
                sr = st["sc"] % 4
                st["sc"] += 1
                sbk = banks[sr]
                pe = None
                nq = len(sp["qk"])
                for bi, (c0, wd, m, sl, q0, fixes) in enumerate(sp["qk"]):
                    waits = [kvev, st["sfree"].get(sr)] if bi == 0 else []
                    lastb = (bi == nq - 1)
                    pe = mm(E, sbk[:, c0:c0 + wd], K[0:67, m, sl * 128:(sl + 1) * 128], Q[0:67, m, q0:q0 + wd],
                            True, not fixes, waits, "pe_m" if (lastb and not fixes) else None, skip=True)
                    for fi, (fc0, ftile) in enumerate(fixes):
                        pe = mm(E, sbk[:, fc0:fc0 + 128], ident_b[:, :], ftile, False, True, [],
                                "pe_m" if (lastb and fi == len(fixes) - 1) else None, skip=True)
                qk_ev[g] = (sr, pe)

            LA = 3
            for g in range(min(LA, len(steps))):
                emit_qk(g)
            pend = []
            fin_ev = [None]

            def flush_fin(min_age):
                keep = []
                for it in pend:
                    age, b, n_, c4 = it
                    if age >= min_age:
                        c5 = E.emit("scalar", lambda e: e.activation(out=rz[b][:, 4:5], in_=rz[b][:, 3:4], func=AF.Ln, bias=epst[:, 1:2], scale=1.0 / 128.0), [c4], "act_a")
                        c6 = E.emit("scalar", lambda e: e.activation(out=rz[b][:, 5:6], in_=rz[b][:, 4:5], func=AF.Exp, scale=-0.5), [c5], "act_a")
                        fin_ev[0] = E.emit("vector", lambda e: e.scalar_tensor_tensor(out=ao[:, n_, :], in0=o1[b][:, :], scalar=rz[b][:, 5:6], in1=gsub[:, :], op0=OP.mult, op1=OP.mult),
                                           [c6, st["aofree"][h % 2]], "dve_b")
                    else:
                        it[0] += 1
                        keep.append(it)
                pend[:] = keep

            for g in range(len(steps)):
                sp = steps[g]
                j = sp["j"]
                sr, pe = qk_ev.pop(g)
                sbk = banks[sr]
                ovs = [banks[4 + 2 * (j % 2) + qt][:, 0:258].rearrange("p (m c) -> p m c", m=2) for qt in range(2)]
                pr = st["pc"] % 3
                st["pc"] += 1
                pt = PT[pr]
                a = E.emit("scalar", lambda e: e.activation(out=pt[:, 0:512], in_=sbk[:, 0:512], func=AF.Exp, bias=kbias[:, h:h + 1], scale=1.0),
                           [pe, st["ptfree"][pr]], "act_b")
                st["sfree"][sr] = a
                pe2 = None
                na = len(sp["av"])
                lastev = {}
                for ai, (pc0, qt, m, sl, first, last) in enumerate(sp["av"]):
                    waits = [a] if ai == 0 else []
                    if first and m == 0:
                        waits = waits + [st["ofree"].get((j % 2, qt))]
                    pe2 = mm(E, ovs[qt][:, m, :], pt[:, pc0:pc0 + 128], V[:, sl, :], first and m == 0, last, waits,
                             "pe_m" if (ai == na - 1 or (last and m == 1)) else None, skip=True)
                    if m == 1:
                        lastev[qt] = pe2
                st["ptfree"][pr] = pe2
                if g + LA < len(steps):
                    emit_qk(g + LA)
                flush_fin(3)
                for (qt, n) in sp["fin"]:
                    ov = ovs[qt]
                    b = st["fc"] % 4
                    st["fc"] += 1
                    d = lambda fn, waits: E.emit("vector", fn, waits, "dve_b")
                    r1 = d(lambda e: e.reciprocal(out=rz[b][:, 0:2], in_=ov[:, :, 128]), [lastev[qt]])
                    r2 = d(lambda e: e.tensor_tensor(out=rz[b][:, 2:3], in0=rz[b][:, 1:2], in1=neglam[:, :], op=OP.mult), [r1])
                    c1 = d(lambda e: e.tensor_scalar(out=o1[b][:, :], in0=ov[:, 0, 0:128], scalar1=rz[b][:, 0:1], scalar2=None, op0=OP.mult), [r2])
                    c2 = d(lambda e: e.scalar_tensor_tensor(out=o1[b][:, :], in0=ov[:, 1, 0:128], scalar=rz[b][:, 2:3], in1=o1[b][:, :], op0=OP.mult, op1=OP.add), [c1])
                    st["ofree"][(j % 2, qt)] = c2
                    c3 = d(lambda e: e.tensor_tensor(out=o2[b][:, :], in0=o1[b][:, :], in1=o1[b][:, :], op=OP.mult), [c2])
                    c4 = d(lambda e: e.reduce_sum(out=rz[b][:, 3:4], in_=o2[b][:, :], axis=AX.X), [c3])
                    pend.append([0, b, n, c4])
            flush_fin(0)
            st["kvfree"][h % 2] = E.last("pe_m")
            st["aofree"][h % 2] = E.dma("sync", as_[:, :, h * 128:(h + 1) * 128].rearrange("n p c -> p n c"), ao[:, :, :], "st_a", [fin_ev[0]])
        barrier(E, [E.last("st_a")])

    def phase3(E):
        ln2_ev = load_ln(E, "ln2_g", "ln2_b")
        wov = wview(w_out)
        st = {"hbfree": None}
        for t in range(NTILE):
            tok0 = t * T
            proj_plan(E, wov, KC, [i * 256 for i in range(8)])
            ffn_plan(E, w2g, w2u, w2d)
            prev = [E.last("pe_m"), E.last("wfree")]
            la = E.dma("sync", a_tok[:, :, :], as_[t * 4:(t + 1) * 4].rearrange("n p c -> p n c"), "ld_a", prev)
            lp = E.dma("sync", mixT[:, 8:16, :], ps[t], "ld_p", prev)
            mevs = [lp]
            for s in range(4):
                evs, pel = tr_in(E, a_tok[:, s, :], la, mixT, s * 128, nchunks=8)
                mevs += evs
            srcs = [mixT[:, c, :] for c in range(KC)]
            fevs = []

            def o_epi(i, r, bks, ev):
                a = E.emit("scalar", lambda e: e.activation(out=fT[:, 2 * i, :], in_=bks[0][:, :], func=AF.Copy), [ev], "act_b")
                d = E.emit("vector", lambda e: e.tensor_copy(out=fT[:, 2 * i + 1, :], in_=bks[1][:, :]), [ev], "dve_b")
                return [a, d]
            proj_run(E, KC, 8, srcs, T, mevs, o_epi)
            fT_evs = [E.last("act_b"), E.last("dve_b")]
            if t == 0:
                E.ln2 = ln2_ev
            else:
                E.ln2 = load_ln(E, "ln2_g", "ln2_b", [E.last("dve_a")])
            h2T_evs = []
            h2st = []

            def cons2(s, ev, zt, zb):
                r0 = tok0 + s * 128
                f1 = E.dma("sync", h2s[r0:r0 + 128, :], zt[:, :], "st_h%d" % zb, [ev])
                h2st.append(f1)
                c = E.emit("scalar", lambda e: e.activation(out=hb[:, :], in_=zt[:, :], func=AF.Copy), [ev, st["hbfree"]], "act_a")
                evs, pel = tr_in(E, hb, c, hT, s * 128)
                st["hbfree"] = pel
                h2T_evs.extend(evs)
                return [f1, c]
            ln_epilogue(E, 4, fT_evs, lambda s: h1s[tok0 + s * 128:tok0 + (s + 1) * 128, :], 1.0 / ALPHA, EPS_S, E.ln2, cons2)
            ln3_ev = load_ln(E, "ln3_g", "ln3_b", [E.last("dve_a")])
            fT_evs = ffn_run(E, T, h2T_evs)

            def cons3(s, ev, zt, zb):
                r0 = tok0 + s * 128
                return [E.dma("sync", y[r0:r0 + 128, :], zt[:, :], "st_y%d" % zb, [ev])]
            ln_epilogue(E, 4, fT_evs, lambda s: h2s[tok0 + s * 128:tok0 + (s + 1) * 128, :], 0.5 / ALPHA, EPS_S, [ln3_ev], cons3, res_waits=h2st)
        ys = [E.last("st_y0"), E.last("st_y1"), E.last("st_y2")]
        E.emit("sync", lambda e: e.nop(), ys)
        barrier(E, ys)

    def program(E):
        prologue(E)
        phase1(E)
        if stage >= 2:
            phase2(E)
        if stage >= 3:
            phase3(E)

    with nc.Block() as blk:
        for eng in ENGS:
            def run(e, eng=eng):
                P = passes[eng]
                P.e = e
                program(P)
            getattr(blk, eng)(run)
    es.close()
    return nc


def _consts():
    p = np.arange(128, dtype=np.float64)
    slopes = 2.0 ** (-(np.arange(NH) + 1.0))
    kbias = (p[:, None] * slopes[None, :]).astype(np.float32)
    corr = np.zeros((128, NH, 128), np.float32)
    k = np.arange(128)[:, None]
    q = np.arange(128)[None, :]
    masked = (k // 64) > (q // 64)
    later = (k > q) & ~masked
    for h in range(NH):
        c = np.where(later, -2.0 * slopes[h] * (k - q), 0.0)
        c = np.where(masked, NEG, c)
        corr[:, h, :] = c
    return kbias, corr.reshape(128, NH * 128), slopes


_NC_CACHE = {}


def kernel(x, meta_tokens, ln1_g, ln1_b, ffn1_w_gate, ffn1_w_up, ffn1_w_down, w_in, lambda_q1, lambda_k1,
           lambda_q2, lambda_k2, subln_g, w_pool, pool_scale, w_out, ln2_g, ln2_b, ffn2_w_gate, ffn2_w_up,
           ffn2_w_down, ln3_g, ln3_b, _stage=3, _debug=False):
    f = lambda a: np.ascontiguousarray(np.asarray(a, dtype=np.float32))
    x = f(x); meta_tokens = f(meta_tokens)
    kbias, corr, slopes = _consts()
    shared = {
        "ident": np.eye(128, dtype=np.float32), "kbias": kbias, "corr": corr,
        "pscale": f(np.asarray(pool_scale).reshape(8, 128).T),
        "ffn1_w_gate": f(ffn1_w_gate)[0], "ffn1_w_up": f(ffn1_w_up)[0], "ffn1_w_down": f(ffn1_w_down)[0],
        "ffn2_w_gate": f(ffn2_w_gate)[0], "ffn2_w_up": f(ffn2_w_up)[0], "ffn2_w_down": f(ffn2_w_down)[0],
        "w_in": f(w_in)[0], "w_out": f(w_out)[0], "w_pool": f(w_pool)[0],
        "ln1_g": f(ln1_g), "ln1_b": f(ln1_b), "ln2_g": f(ln2_g), "ln2_b": f(ln2_b), "ln3_g": f(ln3_g), "ln3_b": f(ln3_b),
        "lambda_q1": f(lambda_q1), "lambda_k1": f(lambda_k1), "lambda_q2": f(lambda_q2), "lambda_k2": f(lambda_k2),
        "subln_g": f(subln_g),
    }
    meta_pad = np.zeros((128, D), np.float32)
    meta_pad[:NMETA] = meta_tokens
    in_maps = []
    for c in range(8):
        b, p = c // 2, c % 2
        tiles = x[b].reshape(64, 128, D)
        own = np.ascontiguousarray(tiles[p::2]).reshape(NOWN, D)
        oth = np.ascontiguousarray(tiles[(1 - p)::2]).reshape(NOWN, D)
        halo = np.zeros((T, D), np.float32)
        halo[:NMETA] = meta_tokens
        for n in range(NQT):
            i = 2 * n + p
            halo[NMETA + 15 * n:NMETA + 15 * n + 15] = meta_tokens[1:16] if i == 0 else x[b, 128 * i - 15:128 * i]
        t = np.arange(NOWN)
        jq = 2 * (t // 128) + p
        qa = np.zeros((NH, 3, NOWN), np.float32)
        for h in range(NH):
            qa[h, 0] = -slopes[h] * ((t % 128) + 16)
            qa[h, 1] = -slopes[h] * 128.0 * jq
            qa[h, 2] = slopes[h]
        ka = np.zeros((3, NKTOK), np.float32)
        ka[0, :NMETA] = 1.0; ka[1, :NMETA] = 1.0; ka[2, :NMETA] = -16.0
        for n in range(NQT):
            for (base, j) in ((1 + n, 2 * n + p), (1 + NQT + n, 2 * n + 1 - p)):
                ka[0, base * 128:(base + 1) * 128] = 1.0
                ka[1, base * 128:(base + 1) * 128] = 1.0
                ka[2, base * 128:(base + 1) * 128] = 128.0 * j
        m = dict(shared)
        m.update({"x_own": own, "x_oth": oth, "x_halo": halo, "qaug": qa, "kaug": ka,
                  "corr2": np.full((128, 128), NEG * (1 - p), np.float32)})
        in_maps.append(m)
    key = (_stage, _debug)
    if key not in _NC_CACHE:
        _NC_CACHE[key] = build(_stage, _debug)
    nc = _NC_CACHE[key]
    res = run_bass_kernel_spmd(nc, in_maps, core_ids=list(range(8)))
    if _debug:
        return res
    out = np.zeros((4, SEQ, D), np.float32)
    for c in range(8):
        b, p = c // 2, c % 2
        out[b].reshape(64, 128, D)[p::2] = res.results[c]["y"].reshape(NQT, 128, D)
    return out
```
